# Optimizing a Trainium2 kernel written in Bass

```python
import jax, jax.numpy as jnp
from jax import lax
import numpy as np

D_MODEL = 2048
BATCH = 4
SEQ = 4096
DEPTH = 1

MEM_TOKENS = 256
HEAD_DIM = 64
N_Q_HEADS = 16
N_KV_HEADS = 4
Q_PER_KV = N_Q_HEADS // N_KV_HEADS
ATTN_WIDTH = N_Q_HEADS * HEAD_DIM
KV_WIDTH = N_KV_HEADS * HEAD_DIM
WINDOW = 128
BLOCK = 128
ROPE_THETA = 10000.0
CONV_WIDTH = 1024
CONV_K = 3
X_HEADS = 4
X_HEAD_DIM = 128
X_WIDTH = X_HEADS * X_HEAD_DIM
FFN_HIDDEN = -(-(8 * D_MODEL) // (3 * 256)) * 256
EPS = 1e-6
IN_SIZES = (ATTN_WIDTH, KV_WIDTH, KV_WIDTH, CONV_WIDTH, CONV_WIDTH, CONV_WIDTH, D_MODEL, D_MODEL)
IN_WIDTH = ATTN_WIDTH + 2 * KV_WIDTH + 3 * CONV_WIDTH + 2 * D_MODEL

kernel_name = "hybrid_gated_swa_shortconv_xattn_block"


def rms_norm(x, g):
    xf = x.astype(jnp.float32)
    y = xf * lax.rsqrt(jnp.mean(xf * xf, axis=-1, keepdims=True) + EPS)
    return (y * g.astype(jnp.float32)).astype(x.dtype)


def rope(x, positions):
    half = HEAD_DIM // 2
    inv_freq = ROPE_THETA ** (-jnp.arange(half, dtype=jnp.float32) / half)
    ang = positions.astype(jnp.float32)[:, None] * inv_freq[None, :]
    cos = jnp.cos(ang)[None, :, None, :]
    sin = jnp.sin(ang)[None, :, None, :]
    xf = x.astype(jnp.float32)
    x1, x2 = xf[..., :half], xf[..., half:]
    out = jnp.concatenate([x1 * cos - x2 * sin, x2 * cos + x1 * sin], axis=-1)
    return out.astype(x.dtype)


def _with_prev_block(t):
    pad = [(0, 0)] * t.ndim
    pad[1] = (1, 0)
    prev = jnp.pad(t, pad)[:, :-1]
    return jnp.concatenate([prev, t], axis=2)


def sliding_window_attention(q, k, v, sinks):
    b, t = q.shape[0], q.shape[1]
    nb = t // BLOCK
    qb = q.reshape(b, nb, BLOCK, N_KV_HEADS, Q_PER_KV, HEAD_DIM)
    kband = _with_prev_block(k.reshape(b, nb, BLOCK, N_KV_HEADS, HEAD_DIM))
    vband = _with_prev_block(v.reshape(b, nb, BLOCK, N_KV_HEADS, HEAD_DIM))
    scale = HEAD_DIM ** -0.5
    s = jnp.einsum('bnqhgd,bnkhd->bnhgqk', qb, kband).astype(jnp.float32) * scale
    blk = jnp.arange(nb)[:, None]
    q_pos = blk * BLOCK + jnp.arange(BLOCK)[None, :]
    k_pos = (blk - 1) * BLOCK + jnp.arange(2 * BLOCK)[None, :]
    diff = q_pos[:, :, None] - k_pos[:, None, :]
    valid = (diff >= 0) & (diff < WINDOW) & (k_pos[:, None, :] >= 0)
    s = jnp.where(valid[None, :, None, None, :, :], s, -jnp.inf)
    sink = sinks.astype(jnp.float32).reshape(N_KV_HEADS, Q_PER_KV)[None, None, :, :, None, None]
    m = jnp.maximum(jnp.max(s, axis=-1, keepdims=True), sink)
    p = jnp.exp(s - m)
    p = p / (jnp.sum(p, axis=-1, keepdims=True) + jnp.exp(sink - m))
    o = jnp.einsum('bnhgqk,bnkhd->bnqhgd', p.astype(v.dtype), vband)
    return o.reshape(b, t, ATTN_WIDTH)


def short_gated_conv(z, gate_b, gate_c, conv_w):
    t = z.shape[1]
    cz = gate_c * z
    zp = jnp.pad(cz, ((0, 0), (CONV_K - 1, 0), (0, 0)))
    y = conv_w[0] * zp[:, 0:t]
    for j in range(1, CONV_K):
        y = y + conv_w[j] * zp[:, j:j + t]
    return gate_b * y


def cross_attention(u, mem_n, w_xq, w_xkv, w_xo):
    b, t = u.shape[0], u.shape[1]
    q = (u @ w_xq).reshape(b, t, X_HEADS, X_HEAD_DIM)
    kv = mem_n @ w_xkv
    k = kv[..., :X_WIDTH].reshape(b, -1, X_HEADS, X_HEAD_DIM)
    v = kv[..., X_WIDTH:].reshape(b, -1, X_HEADS, X_HEAD_DIM)
    s = jnp.einsum('bthd,bmhd->bhtm', q, k).astype(jnp.float32) * (X_HEAD_DIM ** -0.5)
    p = jax.nn.softmax(s, axis=-1)
    o = jnp.einsum('bhtm,bmhd->bthd', p.astype(v.dtype), v).reshape(b, t, X_WIDTH)
    return o @ w_xo


def setup_inputs(seed: int = 0) -> dict:
    key = jax.random.key(seed)
    ks = jax.random.split(key, 20)
    f32 = jnp.float32

    def w(k, shape, fan_in):
        return jax.random.normal(k, shape, f32) * (fan_in ** -0.5)

    def gain(k, shape):
        return 1.0 + 0.01 * jax.random.normal(k, shape, f32)

    return {
        "x": jax.random.normal(ks[0], (BATCH, SEQ, D_MODEL), f32),
        "mem": jax.random.normal(ks[1], (BATCH, MEM_TOKENS, D_MODEL), f32),
        "g_mix": gain(ks[2], (DEPTH, D_MODEL)),
        "w_in": w(ks[3], (DEPTH, D_MODEL, IN_WIDTH), D_MODEL),
        "conv_w": w(ks[4], (DEPTH, CONV_K, CONV_WIDTH), CONV_K),
        "attn_sinks": 0.5 * jax.random.normal(ks[5], (DEPTH, N_Q_HEADS), f32),
        "w_attn_proj": w(ks[6], (DEPTH, ATTN_WIDTH, D_MODEL), ATTN_WIDTH),
        "w_conv_proj": w(ks[7], (DEPTH, CONV_WIDTH, D_MODEL), CONV_WIDTH),
        "w_mix_out": w(ks[8], (DEPTH, D_MODEL, D_MODEL), D_MODEL),
        "g_xattn": gain(ks[9], (DEPTH, D_MODEL)),
        "g_mem": gain(ks[10], (DEPTH, D_MODEL)),
        "w_xq": w(ks[11], (DEPTH, D_MODEL, X_WIDTH), D_MODEL),
        "w_xkv": w(ks[12], (DEPTH, D_MODEL, 2 * X_WIDTH), D_MODEL),
        "w_xo": w(ks[13], (DEPTH, X_WIDTH, D_MODEL), X_WIDTH),
        "g_ffn": gain(ks[14], (DEPTH, D_MODEL)),
        "w_ffn_in": w(ks[15], (DEPTH, D_MODEL, 2 * FFN_HIDDEN), D_MODEL),
        "w_ffn_out": w(ks[16], (DEPTH, FFN_HIDDEN, D_MODEL), FFN_HIDDEN),
        "g_final": gain(ks[17], (D_MODEL,)),
    }


def reference(x, mem, g_mix, w_in, conv_w, attn_sinks, w_attn_proj, w_conv_proj, w_mix_out,
              g_xattn, g_mem, w_xq, w_xkv, w_xo, g_ffn, w_ffn_in, w_ffn_out, g_final):
    b, t = x.shape[0], x.shape[1]
    positions = jnp.arange(t, dtype=jnp.int32)
    split_points = np.cumsum(IN_SIZES)[:-1].tolist()
    h = x
    for l in range(DEPTH):
        u = rms_norm(h, g_mix[l])
        proj = u @ w_in[l]
        q, k, v, z, gb, gc, gate_a, gate_c = jnp.split(proj, split_points, axis=-1)
        q = rope(q.reshape(b, t, N_Q_HEADS, HEAD_DIM), positions)
        k = rope(k.reshape(b, t, N_KV_HEADS, HEAD_DIM), positions)
        v = v.reshape(b, t, N_KV_HEADS, HEAD_DIM)
        y_attn = sliding_window_attention(q, k, v, attn_sinks[l]) @ w_attn_proj[l]
        y_conv = short_gated_conv(z, gb, gc, conv_w[l]) @ w_conv_proj[l]
        merged = jax.nn.sigmoid(gate_a) * y_attn + jax.nn.sigmoid(gate_c) * y_conv
        h = h + merged @ w_mix_out[l]
        u = rms_norm(h, g_xattn[l])
        mem_n = rms_norm(mem, g_mem[l])
        h = h + cross_attention(u, mem_n, w_xq[l], w_xkv[l], w_xo[l])
        u = rms_norm(h, g_ffn[l])
        hid = u @ w_ffn_in[l]
        h = h + (jax.nn.silu(hid[..., :FFN_HIDDEN]) * hid[..., FFN_HIDDEN:]) @ w_ffn_out[l]
    return rms_norm(h, g_final)
```

```python
import numpy as np
import ml_dtypes
import concourse.bass as bass
import concourse.mybir as mybir
from concourse.bass_utils import run_bass_kernel_spmd

F32 = mybir.dt.float32
BF16 = mybir.dt.bfloat16
U8 = mybir.dt.uint8
AF = mybir.ActivationFunctionType
ALU = mybir.AluOpType

UNIT = 512


class _Op:
    __slots__ = ("eng", "fn", "deps_c", "deps_d", "dma", "sig", "seq", "idx")


class Prog:
    ENGS = ("pe", "act", "dve", "pool", "sp")

    def __init__(self):
        self.ops = []
        self.last_w = {}
        self.readers = {}
        self.dma_count = {}

    def op(self, eng, fn, reads=(), writes=(), dma=None):
        o = _Op()
        o.eng, o.fn, o.idx = eng, fn, len(self.ops)
        o.dma = None
        o.sig = False
        o.seq = 0
        deps = set()
        lw, rd = self.last_w, self.readers
        for t in reads:
            w = lw.get(t)
            if w is not None:
                deps.add(w)
        for t in writes:
            w = lw.get(t)
            if w is not None:
                deps.add(w)
            r = rd.get(t)
            if r:
                deps.update(r.values())
        for t in writes:
            lw[t] = o.idx
            rd[t] = {}
        for t in reads:
            r = rd.get(t)
            if r is None:
                r = rd[t] = {}
            r[eng if dma is None else ("dma", o.idx)] = o.idx
        deps.discard(o.idx)
        dc, dd = {}, {}
        for d in deps:
            p = self.ops[d]
            if p.dma is not None:
                k, c = p.dma
                if dd.get(k, 0) < c:
                    dd[k] = c
            else:
                if p.eng == "pe" and eng == "pe" and dma is None:
                    continue
                if dc.get(p.eng, -1) < d:
                    dc[p.eng] = d
        o.deps_c, o.deps_d = dc, dd
        if dma is not None:
            c = self.dma_count.get(dma, 0) + 1
            self.dma_count[dma] = c
            o.dma = (dma, 16 * c)
        self.ops.append(o)
        return o

    def emit(self, nc, final_wait_keys=()):
        ops = self.ops
        for o in ops:
            for d in o.deps_c.values():
                ops[d].sig = True
        cnt = {e: 0 for e in self.ENGS}
        for o in ops:
            if o.dma is None and o.sig:
                cnt[o.eng] += 1
                o.seq = cnt[o.eng]
        from contextlib import ExitStack
        with ExitStack() as st:
            esem = {e: st.enter_context(nc.semaphore("s_" + e)) for e in self.ENGS}
            dsem = {k: st.enter_context(nc.semaphore("d_%d" % i)) for i, k in enumerate(self.dma_count)}
            block = st.enter_context(nc.Block())
            per = {e: [o for o in ops if o.eng == e] for e in self.ENGS}

            def body(ename):
                def run(e):
                    wc = {x: 0 for x in self.ENGS}
                    wd = {}
                    for o in per[ename]:
                        for pe_, d in o.deps_c.items():
                            need = ops[d].seq
                            if wc[pe_] < need:
                                e.wait_ge(esem[pe_], need)
                                wc[pe_] = need
                        for k, c in o.deps_d.items():
                            if wd.get(k, 0) < c:
                                e.wait_ge(dsem[k], c)
                                wd[k] = c
                        ins = o.fn(e)
                        if o.dma is not None:
                            ins.then_inc(dsem[o.dma[0]], 16)
                        elif o.sig:
                            ins.then_inc(esem[ename], 1)
                    if ename == "sp":
                        for k in final_wait_keys:
                            e.wait_ge(dsem[k], 16 * self.dma_count[k])
                return run

            block.tensor(body("pe"))
            block.scalar(body("act"))
            block.vector(body("dve"))
            block.gpsimd(body("pool"))
            block.sync(body("sp"))


D = 2048
SEQ = 4096
BATCH = 4
NCORES = 8
TOK_CORE = 2048
NPASS = 2
T = 1024
NT = 8
TH = T + 128
FFN = 5632
EPS = 1e-6
NSLOT = 4
SLOT_BYTES = 8192


class Ref:
    __slots__ = ("ap", "toks")

    def __init__(self, ap, toks):
        self.ap, self.toks = ap, toks


class Buf:
    def __init__(self, sb, off, dt, shape):
        self.sb, self.off, self.dt, self.shape = sb, off, dt, tuple(shape)
        self.esz = 2 if dt == BF16 else 4
        n = 1
        for s in shape:
            n *= s
        self.n = n
        self.nbytes = n * self.esz
        full = sb[:, off:off + self.nbytes].bitcast(dt)
        if len(shape) == 2:
            full = full.rearrange("p (a b) -> p a b", a=shape[0])
        elif len(shape) == 3:
            full = full.rearrange("p (a b c) -> p a b c", a=shape[0], b=shape[1])
        self.full = full

    def sl(self, *idx, p=(0, 128)):
        shape = self.shape
        idx = list(idx) + [slice(None)] * (len(shape) - len(idx))
        rng = []
        for i, s in zip(idx, shape):
            if isinstance(i, int):
                rng.append((i, i + 1))
            else:
                a = 0 if i.start is None else i.start
                b = s if i.stop is None else i.stop
                rng.append((a, b))
        ap = self.full[(slice(p[0], p[1]),) + tuple(idx)]
        strides = [1] * len(shape)
        for d in range(len(shape) - 2, -1, -1):
            strides[d] = strides[d + 1] * shape[d + 1]
        toks = set()
        lead = rng[:-1]
        la, lb = rng[-1]

        def rec(d, base):
            if d == len(lead):
                b0 = self.off + (base + la) * self.esz
                b1 = self.off + (base + lb) * self.esz - 1
                for u in range(b0 // UNIT, b1 // UNIT + 1):
                    toks.add(u)
                return
            for i in range(lead[d][0], lead[d][1]):
                rec(d + 1, base + i * strides[d])

        rec(0, 0)
        return Ref(ap, list(toks))


class _Stop(Exception):
    pass


def _build(seq_in=None, stop=None):
    nc = bass.Bass("TRN2", target_bir_lowering=False)
    P = Prog()
    dram = {}

    def din(name, shape, dt=F32):
        dram[name] = nc.dram_tensor(name, list(shape), dt, kind="ExternalInput").ap()
        return dram[name]

    xin = din("xin", [NPASS, TH, D])
    memin = din("memin", [256, D])
    gvec = din("gvec", [5, D])
    w_in = din("w_in", [D, 8704])
    w_ap = din("w_ap", [1024, D])
    w_cp = din("w_cp", [1024, D])
    w_mix = din("w_mix", [D, D])
    w_xq = din("w_xq", [D, 512])
    w_xkv = din("w_xkv", [D, 1024])
    w_xo = din("w_xo", [512, D])
    w_f1 = din("w_f1", [D, 2 * FFN])
    w_f2 = din("w_f2", [FFN, D])
    cwin = din("cwin", [128, 24])
    sinkin = din("sinkin", [1, 16])
    ropein = din("ropein", [NPASS, 128, 2, 9, 32])
    maskin = din("maskin", [128, 3, 2, 128], BF16)
    identin = din("identin", [128, 128], BF16)
    yout = nc.dram_tensor("yout", [TOK_CORE, D], F32, kind="ExternalOutput").ap()
    dbg = nc.dram_tensor("dbg", [128, 206 * 1024], U8, kind="ExternalOutput").ap() if stop else None
    memmap = {}
    wd = {"w_in": w_in, "w_ap": w_ap, "w_cp": w_cp, "w_mix": w_mix, "w_xq": w_xq, "w_xkv": w_xkv,
          "w_xo": w_xo, "w_f1": w_f1, "w_f2": w_f2}
    wview = {k: v.rearrange("(k p) c -> p k c", p=128) for k, v in wd.items()}

    from contextlib import ExitStack
    st = ExitStack()
    SBYTES = 206 * 1024
    sb = st.enter_context(nc.sbuf_tensor("sb", [128, SBYTES], U8))
    ps = st.enter_context(nc.psum_tensor("ps", [128, 8, 512], F32))

    off = [0]

    def alloc(nbytes, align=64):
        o = (off[0] + align - 1) // align * align
        off[0] = o + nbytes
        assert off[0] <= SBYTES, ("SBUF overflow", off[0])
        return o

    O_H = alloc(NT * 8192, 512)
    O_UT = alloc(16 * TH * 2, 512)
    O_R3 = alloc(32768, 512)
    O_WS = alloc(NSLOT * SLOT_BYTES, 512)
    O_SCR = alloc(16384, 512)
    O_GB = alloc(8192, 512)
    O_UB = alloc(4096, 512)
    O_ROPE = alloc(2 * 9 * 32 * 4, 512)
    O_MASK = alloc(3 * 2 * 128 * 2, 512)
    O_ID = alloc(256, 512)
    O_ONES = alloc(256, 512)
    O_MEMK = alloc(4 * 256 * 2, 512)
    O_MEMV = alloc(2 * 512 * 2, 512)
    O_SS = alloc(80 * 4, 512)
    O_RS = alloc(80 * 4, 512)
    O_CW = alloc(24 * 4, 512)
    O_SINK = alloc(16 * 4, 64)
    O_SK2 = alloc(2 * 4 * 4, 64)
    O_I4 = alloc(1024, 512)
    O_ONESP = alloc(512, 512)

    Hb = Buf(sb, O_H, F32, [NT, D])
    UT = Buf(sb, O_UT, BF16, [16, TH])
    GB = Buf(sb, O_GB, F32, [D])
    UB = Buf(sb, O_UB, BF16, [D])
    ROPE = Buf(sb, O_ROPE, F32, [2, 9, 32])
    MASK = Buf(sb, O_MASK, BF16, [3, 2, 128])
    IDT = Buf(sb, O_ID, BF16, [128])
    ONES = Buf(sb, O_ONES, BF16, [128])
    MEMK = Buf(sb, O_MEMK, BF16, [4, 256])
    MEMV = Buf(sb, O_MEMV, BF16, [2, 512])
    SS = Buf(sb, O_SS, F32, [80])
    RS = Buf(sb, O_RS, F32, [80])
    CW = Buf(sb, O_CW, F32, [24])
    SINK = Buf(sb, O_SINK, F32, [16])
    SK2 = Buf(sb, O_SK2, F32, [2, 4])
    I4 = Buf(sb, O_I4, BF16, [512])
    ONESP = Buf(sb, O_ONESP, BF16, [2, 128])
    AO = Buf(sb, O_H + 4 * 8192, BF16, [8, T])
    CV = Buf(sb, O_H + 6 * 8192, BF16, [8, T])
    QT = Buf(sb, O_R3, BF16, [8, T])
    KT = Buf(sb, O_R3 + 16384, BF16, [2, TH])
    VA = Buf(sb, O_R3 + 16384 + 4608, BF16, [9, 4, 128])
    MG = Buf(sb, O_R3, BF16, [16, T])
    XQ = Buf(sb, O_R3, BF16, [4, T])
    XO = Buf(sb, O_R3 + 8192, BF16, [4, T])
    ACTb = [Buf(sb, O_R3 + i * 16384, BF16, [8, T]) for i in range(2)]
    XH = Buf(sb, O_SCR, F32, [D])
    JUNK = Buf(sb, O_SCR + 8192, BF16, [D])

    def psr(b, c0=0, c1=512, p=(0, 128)):
        return Ref(ps[p[0]:p[1], b, c0:c1], [("ps", b)])

    def psr2(b, p=(0, 128)):
        return Ref(ps[p[0]:p[1], b:b + 2, :].rearrange("p a b -> p (a b)"), [("ps", b), ("ps", b + 1)])

    def psbf(b, c0, c1):
        return Ref(ps[:, b, :].bitcast(BF16)[:, c0:c1], [("ps", b)])

    bank_ctr = [0]

    def banks(n=1):
        b = bank_ctr[0]
        if n == 2 and b % 2:
            b += 1
        if b + n > 8:
            b = 0
        bank_ctr[0] = (b + n) % 8
        return b

    def mm(out, lhsT, rhs, start, stop):
        P.op("pe", lambda e: e.matmul(out.ap, lhsT=lhsT.ap, rhs=rhs.ap, start=start, stop=stop),
             reads=lhsT.toks + rhs.toks, writes=out.toks)

    def tr(out, in_):
        P.op("pe", lambda e: e.transpose(out=out.ap, in_=in_.ap, identity=IDT.full),
             reads=in_.toks + IDT.sl().toks, writes=out.toks)

    def act(out, in_, func, scale=1.0, bias=0.0, accum=None):
        sc = scale.ap if isinstance(scale, Ref) else scale
        rd = in_.toks + (scale.toks if isinstance(scale, Ref) else [])
        wr = out.toks + (accum.toks if accum is not None else [])
        if accum is None:
            P.op("act", lambda e: e.activation(out=out.ap, in_=in_.ap, func=func, bias=bias, scale=sc),
                 reads=rd, writes=wr)
        else:
            P.op("act", lambda e: e.activation(out=out.ap, in_=in_.ap, func=func, bias=bias, scale=sc,
                                               accum_out=accum.ap), reads=rd, writes=wr)

    def cp(eng, out, in_):
        if eng == "act":
            P.op("act", lambda e: e.copy(out=out.ap, in_=in_.ap), reads=in_.toks, writes=out.toks)
        else:
            P.op(eng, lambda e: e.tensor_copy(out=out.ap, in_=in_.ap), reads=in_.toks, writes=out.toks)

    def tt(eng, out, in0, in1, op):
        P.op(eng, lambda e: e.tensor_tensor(out=out.ap, in0=in0.ap, in1=in1.ap, op=op),
             reads=in0.toks + in1.toks, writes=out.toks)

    def ts(eng, out, in0, s1, s2, op0, op1=None):
        a1 = s1.ap if isinstance(s1, Ref) else s1
        a2 = s2.ap if isinstance(s2, Ref) else s2
        rd = in0.toks + (s1.toks if isinstance(s1, Ref) else []) + (s2.toks if isinstance(s2, Ref) else [])
        if op1 is None:
            P.op(eng, lambda e: e.tensor_scalar(out=out.ap, in0=in0.ap, scalar1=a1, scalar2=None, op0=op0),
                 reads=rd, writes=out.toks)
        else:
            P.op(eng, lambda e: e.tensor_scalar(out=out.ap, in0=in0.ap, scalar1=a1, scalar2=a2, op0=op0, op1=op1),
                 reads=rd, writes=out.toks)

    def stt(eng, out, in0, scalar, in1, op0, op1):
        a = scalar.ap if isinstance(scalar, Ref) else scalar
        rd = in0.toks + in1.toks + (scalar.toks if isinstance(scalar, Ref) else [])
        P.op(eng, lambda e: e.scalar_tensor_tensor(out=out.ap, in0=in0.ap, scalar=a, in1=in1.ap, op0=op0, op1=op1),
             reads=rd, writes=out.toks)

    def memset(eng, out, val):
        P.op(eng, lambda e: e.memset(out.ap, val), writes=out.toks)

    dma_ctr = [0]

    def dma(eng, out_ap, in_ap, reads=(), writes=(), key=None):
        if key is None:
            dma_ctr[0] += 1
            key = ("m", dma_ctr[0] % 6)
        P.op(eng, lambda e: e.dma_start(out=out_ap, in_=in_ap), reads=list(reads), writes=list(writes), dma=key)

    RING = NSLOT * SLOT_BYTES
    seq_rec = []
    ws = {"get": 0, "emit": 0, "rel": 0}
    ws_offs, ws_need = [], []
    if seq_in is not None:
        head = 0
        for (_n, _k0, nk, _c0, ncols) in seq_in:
            size = nk * ncols * 2
            o = (head + size - 1) // size * size
            if o + size > RING:
                o = 0
            ws_offs.append(o)
            head = o + size
        for j, dj in enumerate(seq_in):
            sj = dj[2] * dj[4] * 2
            nd = -1
            for i in range(j - 1, max(-1, j - 40), -1):
                si = seq_in[i][2] * seq_in[i][4] * 2
                if ws_offs[i] < ws_offs[j] + sj and ws_offs[j] < ws_offs[i] + si:
                    nd = i
                    break
            ws_need.append(nd)

    def ws_dma(name, k0, nk, c0, ncols, o):
        b = Buf(sb, O_WS + o, BF16, [nk, ncols])
        r = b.sl()
        src = wview[name][:, k0:k0 + nk, c0:c0 + ncols]
        P.op("pool", lambda e: e.dma_start(out=r.ap, in_=src), writes=r.toks, dma=("ws", o // 2048))
        return b

    def ws_pump():
        while ws["emit"] < len(seq_in) and ws_need[ws["emit"]] < ws["rel"]:
            j = ws["emit"]
            ws_dma(*seq_in[j], ws_offs[j])
            ws["emit"] = j + 1

    def ws_get(name, k0, nk, c0, ncols):
        assert nk * ncols * 2 <= SLOT_BYTES
        d = (name, k0, nk, c0, ncols)
        seq_rec.append(d)
        j = ws["get"]
        ws["get"] = j + 1
        if seq_in is None:
            return ws_dma(*d, 0)
        assert seq_in[j] == d, (j, seq_in[j], d)
        ws_pump()
        assert ws["emit"] > j, ("weight ring too small for slabs held at once", j)
        return Buf(sb, O_WS + ws_offs[j], BF16, [nk, ncols])

    def ws_done(n=1):
        ws["rel"] += n
        if seq_in is not None:
            ws_pump()

    dma("sp", IDT.full, identin, writes=IDT.sl().toks, key="c0")
    dma("sp", MASK.full, maskin, writes=MASK.sl().toks, key="c1")
    dma("sp", CW.full, cwin, writes=CW.sl().toks, key="c2")
    dma("sp", SINK.full, sinkin[0:1, :].broadcast_to([128, 16]), writes=SINK.sl().toks, key="c3")
    memset("dve", ONES.sl(), 1.0)
    memset("dve", SS.sl(), 0.0)
    act(SINK.sl(), SINK.sl(), AF.Exp)
    for kc_ in range(2):
        for hh_ in range(2):
            pr_ = (64 * hh_, 64 * hh_ + 64)
            hk_ = 2 * kc_ + hh_
            cp("dve", SK2.sl(kc_, p=pr_), SINK.sl(slice(hk_ * 4, hk_ * 4 + 4), p=pr_))
    for g_ in range(4):
        cp("dve", I4.sl(slice(g_ * 128, g_ * 128 + 128)), IDT.sl())
    memset("dve", ONESP.sl(), 0.0)
    memset("dve", ONESP.sl(0, slice(0, 64)), 1.0)
    memset("dve", ONESP.sl(1, slice(64, 128)), 1.0)
    stat_ctr = [0]

    def load_g(i):
        dma("sp", GB.full, gvec[i:i + 1, :].broadcast_to([128, D]), writes=GB.sl().toks, key="g")

    UB2 = Buf(sb, O_SCR + 12288, BF16, [D])

    def norm_rstd(xts):
        n = len(xts)
        c0 = stat_ctr[0]
        stat_ctr[0] += n
        for i, xt in enumerate(xts):
            act(JUNK.sl(), xt, AF.Square, accum=SS.sl(slice(c0 + i, c0 + i + 1)))
        blk = RS.sl(slice(c0, c0 + n))
        ts("dve", blk, SS.sl(slice(c0, c0 + n)), 1.0 / D, EPS, ALU.mult, ALU.add)
        act(blk, blk, AF.Sqrt)
        P.op("dve", lambda e: e.reciprocal(out=blk.ap, in_=blk.ap), reads=blk.toks, writes=blk.toks)
        return [RS.sl(slice(c0 + i, c0 + i + 1)) for i in range(n)]

    def norm_rstd1(xt):
        c = stat_ctr[0]
        stat_ctr[0] += 1
        ssc = SS.sl(slice(c, c + 1))
        rsc = RS.sl(slice(c, c + 1))
        act(JUNK.sl(), xt, AF.Square, accum=ssc)
        ts("dve", rsc, ssc, 1.0 / D, EPS, ALU.mult, ALU.add)
        act(rsc, rsc, AF.Sqrt)
        P.op("dve", lambda e: e.reciprocal(out=rsc.ap, in_=rsc.ap), reads=rsc.toks, writes=rsc.toks)
        return rsc

    def norm_phase(xts, cols):
        n = len(xts)
        rs = [None] * n

        def body(i):
            xt, col0 = xts[i], cols[i]
            ub = UB if i % 2 == 0 else UB2
            stt("dve", ub.sl(), xt, rs[i], GB.sl(), ALU.mult, ALU.mult)
            for half in range(2):
                b = banks(1)
                for j in range(8):
                    k = half * 8 + j
                    tr(psbf(b, j * 128, (j + 1) * 128), ub.sl(slice(k * 128, (k + 1) * 128)))
                src = Ref(ps[:, b, :].bitcast(BF16).rearrange("p (a b) -> p a b", a=8), [("ps", b)])
                cp("act", UT.sl(slice(half * 8, half * 8 + 8), slice(col0, col0 + 128)), src)

        for i in range(n + 1):
            if i < n:
                rs[i] = norm_rstd1(xts[i])
            if i >= 1:
                body(i - 1)

    load_g(2)
    for mt in range(2):
        dma("sp", XH.full, memin[mt * 128:(mt + 1) * 128, :], writes=XH.sl().toks, key="xh")
        norm_phase([XH.sl()], [mt * 128])
    for s in range(2):
        slab = ws_get("w_xkv", 0, 16, s * 256, 256)
        for hh in range(2):
            hx = s * 2 + hh
            b = banks(1)
            for k in range(16):
                mm(psr(b, 0, 256), slab.sl(k, slice(hh * 128, hh * 128 + 128)), UT.sl(k, slice(0, 256)), k == 0, k == 15)
            cp("act", MEMK.sl(hx), psr(b, 0, 256))
        ws_done()
    for s in range(2):
        slab = ws_get("w_xkv", 0, 16, 512 + s * 256, 256)
        for mt in range(2):
            b = banks(1)
            for k in range(16):
                mm(psr(b, 0, 256), UT.sl(k, slice(mt * 128, mt * 128 + 128)), slab.sl(k), k == 0, k == 15)
            cp("dve", MEMV.sl(mt, slice(s * 256, s * 256 + 256)), psr(b, 0, 256))
        ws_done()

    out_keys = []

    def chk(name, ps_):
        if stop == (name, ps_):
            allt = list(range(0, SBYTES // UNIT))
            dma("sp", dbg, sb[:, :], reads=allt, key="dbg")
            out_keys.append("dbg")
            raise _Stop()

    memmap.update(dict(O_H=O_H, O_UT=O_UT, O_R3=O_R3, O_SCR=O_SCR, O_MEMK=O_MEMK, O_MEMV=O_MEMV, O_RS=O_RS, O_SS=O_SS,
                       O_SINK=O_SINK, O_GB=O_GB, O_UB=O_UB, O_ROPE=O_ROPE, O_MASK=O_MASK))
    try:
      for ps_ in range(NPASS):
          dma("sp", ROPE.full, ropein[ps_], writes=ROPE.sl().toks, key="rope")
          load_g(0)
          dma("sp", XH.full, xin[ps_, 0:128, :], writes=XH.sl().toks, key="xh")
          for t in range(NT):
              dma("sp", Hb.full[:, t, :], xin[ps_, 128 + t * 128:256 + t * 128, :], writes=Hb.sl(t).toks, key=("x", t))
          norm_phase([XH.sl()] + [Hb.sl(t) for t in range(NT)], [128 * t for t in range(NT + 1)])

          chk("A", ps_)
          memset("dve", VA.sl(), 0.0)
          SC_A = Buf(sb, O_SCR, F32, [2, 256])
          SC_B = Buf(sb, O_SCR + 2048, F32, [2, 256])
          SC_Q = Buf(sb, O_SCR + 4096, BF16, [2, 256])
          it = 0
          pend = []
          for s in range(6):
              slab = ws_get("w_in", 0, 16, s * 256, 256)
              for t in range(9):
                  if s < 4 and t == 0:
                      continue
                  b = banks(1)
                  for k in range(16):
                      mm(psr(b, 0, 256), UT.sl(k, slice(t * 128, t * 128 + 128)), slab.sl(k), k == 0, k == 15)
                  if s == 5:
                      pv = ps[:, b, 0:256].rearrange("p (a b c) -> p a b c", a=2, b=2)
                      va = VA.full[:, t, :, :].rearrange("p (a b) c -> p a b c", a=2)
                      P.op("act", lambda e, pv=pv, va=va: e.copy(out=va[:, :, 0, 0:64], in_=pv[:, :, 0, :]),
                           reads=[("ps", b)], writes=VA.sl(t).toks)
                      P.op("dve", lambda e, pv=pv, va=va: e.tensor_copy(out=va[:, :, 1, 64:128], in_=pv[:, :, 1, :]),
                           reads=[("ps", b)], writes=VA.sl(t).toks)
                      continue
                  i2 = it % 2
                  it += 1
                  A_ = SC_A.sl(i2)
                  B_ = SC_B.sl(i2)
                  Q_ = SC_Q.sl(i2)
                  pq = ps[:, b, 0:256].rearrange("p (h two f) -> p h two f", h=4, two=2)
                  cosb = ROPE.full[:, 0, t, :].unsqueeze(1).unsqueeze(1).broadcast_to([128, 4, 2, 32])
                  sinb = ROPE.full[:, 1, t, :].unsqueeze(1).broadcast_to([128, 4, 32])
                  Aap = SC_A.full[:, i2, :].rearrange("p (h two f) -> p h two f", h=4, two=2)
                  Bap = SC_B.full[:, i2, :].rearrange("p (h two f) -> p h two f", h=4, two=2)
                  rt = ROPE.sl().toks
                  P.op("dve", lambda e, Aap=Aap, pq=pq, cosb=cosb: e.tensor_tensor(out=Aap, in0=pq, in1=cosb, op=ALU.mult),
                       reads=[("ps", b)] + rt, writes=A_.toks)
                  P.op("dve", lambda e, Bap=Bap, pq=pq, sinb=sinb: e.scalar_tensor_tensor(
                      out=Bap[:, :, 0, :], in0=pq[:, :, 1, :], scalar=-1.0, in1=sinb, op0=ALU.mult, op1=ALU.mult),
                      reads=[("ps", b)] + rt, writes=B_.toks)
                  P.op("dve", lambda e, Bap=Bap, pq=pq, sinb=sinb: e.tensor_tensor(
                      out=Bap[:, :, 1, :], in0=pq[:, :, 0, :], in1=sinb, op=ALU.mult),
                      reads=[("ps", b)] + rt, writes=B_.toks)
                  tt("dve", Q_, A_, B_, ALU.add)

                  def fin(s=s, t=t, i2=i2):
                      b2 = banks(1)
                      for j in range(2):
                          tr(psbf(b2, j * 128, (j + 1) * 128), SC_Q.sl(i2, slice(j * 128, (j + 1) * 128)))
                      src = Ref(ps[:, b2, :].bitcast(BF16)[:, 0:256].rearrange("p (a b) -> p a b", a=2), [("ps", b2)])
                      if s < 4:
                          cp("act", QT.sl(slice(2 * s, 2 * s + 2), slice((t - 1) * 128, t * 128)), src)
                      else:
                          cp("act", KT.sl(slice(0, 2), slice(t * 128, (t + 1) * 128)), src)
                  pend.append(fin)
                  if len(pend) > 1:
                      pend.pop(0)()
              ws_done()
          while pend:
              pend.pop(0)()

          chk("B", ps_)
          PT = [Buf(sb, O_SCR + i * 4096, BF16, [2, 2, 512]) for i in range(2)]
          RC = [Buf(sb, O_SCR + 8192 + i * 2048, F32, [512]) for i in range(2)]
          iters = [(n, kc) for n in range(NT) for kc in range(2)]

          def qk(n, kc, i2):
            mi = (1 + ps_) if n == 0 else 0
            for hh in range(2):
                pr = (64 * hh, 64 * hh + 64)
                for j in range(2):
                    bk = 2 * hh + j
                    mm(psr(bk), KT.sl(kc, slice((n + j) * 128, (n + j + 1) * 128), p=pr),
                       QT.sl(slice(4 * kc, 4 * kc + 4), slice(n * 128, (n + 1) * 128), p=pr), True, False)
                    mm(psr(bk), MASK.sl(mi, j), I4.sl(), False, True)
                act(PT[i2].sl(hh), Ref(ps[:, 2 * hh:2 * hh + 2, :], [("ps", 2 * hh), ("ps", 2 * hh + 1)]), AF.Exp, scale=0.125)

          def pv(n, kc, i2):
            bo = 4 + 2 * i2
            for q_ in range(4):
                hh, j = q_ // 2, q_ % 2
                mm(psr(bo), VA.sl(n + j, 2 * kc + hh), PT[i2].sl(hh, j), q_ == 0, q_ == 3)
            for q_ in range(4):
                hh, j = q_ // 2, q_ % 2
                mm(psr(bo + 1), ONESP.sl(hh), PT[i2].sl(hh, j), q_ == 0, q_ == 3)
            skb = SK2.full[:, kc, :].unsqueeze(2).broadcast_to([128, 4, 128])
            rc = RC[i2].full.rearrange("p (g q) -> p g q", g=4)
            pd = ps[:, bo + 1, :].rearrange("p (g q) -> p g q", g=4)
            P.op("dve", lambda e: e.tensor_tensor(out=rc, in0=pd, in1=skb, op=ALU.add),
                 reads=[("ps", bo + 1)] + SK2.sl().toks, writes=RC[i2].sl().toks)
            act(RC[i2].sl(), RC[i2].sl(), AF.Ln)
            act(RC[i2].sl(), RC[i2].sl(), AF.Exp, scale=-1.0)
            po = ps[:, bo, :].rearrange("p (g q) -> p g q", g=4)
            ao = AO.sl(slice(4 * kc, 4 * kc + 4), slice(n * 128, (n + 1) * 128))
            P.op("dve", lambda e: e.tensor_tensor(out=ao.ap, in0=po, in1=rc, op=ALU.mult),
                 reads=[("ps", bo)] + RC[i2].sl().toks, writes=ao.toks)

          for i, (n, kc) in enumerate(iters):
              qk(n, kc, i % 2)
              if i > 0:
                  pv(*iters[i - 1], (i - 1) % 2)
          pv(*iters[-1], (len(iters) - 1) % 2)
          bank_ctr[0] = 0

          chk("C", ps_)
          ZS = Buf(sb, O_SCR, F32, [1026])
          CZ = Buf(sb, O_SCR + 4608, F32, [1026])
          AC = Buf(sb, O_SCR + 9216, F32, [1024])
          for c in range(8):
              wcol = slice(0, 128)
              slz = ws_get("w_in", 0, 16, 1536 + c * 128, 128)
              bz = banks(2)
              bzh = banks(1)
              for k in range(16):
                  w = slz.sl(k, wcol)
                  mm(psr(bz), w, UT.sl(k, slice(128, 640)), k == 0, k == 15)
                  mm(psr(bz + 1), w, UT.sl(k, slice(640, 1152)), k == 0, k == 15)
                  mm(psr(bzh, 0, 2), w, UT.sl(k, slice(126, 128)), k == 0, k == 15)
              ws_done()
              cp("act", ZS.sl(slice(2, 1026)), psr2(bz))
              cp("act", ZS.sl(slice(0, 2)), psr(bzh, 0, 2))
              slg = ws_get("w_in", 0, 16, 3584 + c * 128, 128)
              bg = banks(2)
              bgh = banks(1)
              for k in range(16):
                  w = slg.sl(k, wcol)
                  mm(psr(bg), w, UT.sl(k, slice(128, 640)), k == 0, k == 15)
                  mm(psr(bg + 1), w, UT.sl(k, slice(640, 1152)), k == 0, k == 15)
                  mm(psr(bgh, 0, 2), w, UT.sl(k, slice(126, 128)), k == 0, k == 15)
              ws_done()
              tt("dve", CZ.sl(slice(2, 1026)), psr2(bg), ZS.sl(slice(2, 1026)), ALU.mult)
              tt("dve", CZ.sl(slice(0, 2)), psr(bgh, 0, 2), ZS.sl(slice(0, 2)), ALU.mult)
              slb = ws_get("w_in", 0, 16, 2560 + c * 128, 128)
              bb = banks(2)
              for k in range(16):
                  w = slb.sl(k, wcol)
                  mm(psr(bb), w, UT.sl(k, slice(128, 640)), k == 0, k == 15)
                  mm(psr(bb + 1), w, UT.sl(k, slice(640, 1152)), k == 0, k == 15)
              ws_done()
              act(AC.sl(), CZ.sl(slice(0, 1024)), AF.Copy, scale=CW.sl(slice(c * 3, c * 3 + 1)))
              stt("dve", AC.sl(), CZ.sl(slice(1, 1025)), CW.sl(slice(c * 3 + 1, c * 3 + 2)), AC.sl(), ALU.mult, ALU.add)
              stt("dve", AC.sl(), CZ.sl(slice(2, 1026)), CW.sl(slice(c * 3 + 2, c * 3 + 3)), AC.sl(), ALU.mult, ALU.add)
              tt("dve", CV.sl(c), psr2(bb), AC.sl(), ALU.mult)

          chk("D", ps_)
          SGA = Buf(sb, O_SCR, F32, [1024])
          M1 = Buf(sb, O_SCR + 4096, F32, [1024])
          SGC = Buf(sb, O_SCR + 8192, F32, [1024])
          for f in range(16):
              wcol = slice(0, 128)
              sla = ws_get("w_in", 0, 16, 4608 + f * 128, 128)
              b = banks(2)
              for k in range(16):
                  w = sla.sl(k, wcol)
                  mm(psr(b), w, UT.sl(k, slice(128, 640)), k == 0, k == 15)
                  mm(psr(b + 1), w, UT.sl(k, slice(640, 1152)), k == 0, k == 15)
              ws_done()
              act(SGA.sl(), psr2(b), AF.Sigmoid)
              slp = ws_get("w_ap", 0, 8, f * 128, 128)
              b = banks(2)
              for k in range(8):
                  w = slp.sl(k, wcol)
                  mm(psr(b), w, AO.sl(k, slice(0, 512)), k == 0, k == 7)
                  mm(psr(b + 1), w, AO.sl(k, slice(512, 1024)), k == 0, k == 7)
              ws_done()
              tt("dve", M1.sl(), psr2(b), SGA.sl(), ALU.mult)
              slc = ws_get("w_in", 0, 16, 6656 + f * 128, 128)
              b = banks(2)
              for k in range(16):
                  w = slc.sl(k, wcol)
                  mm(psr(b), w, UT.sl(k, slice(128, 640)), k == 0, k == 15)
                  mm(psr(b + 1), w, UT.sl(k, slice(640, 1152)), k == 0, k == 15)
              ws_done()
              act(SGC.sl(), psr2(b), AF.Sigmoid)
              slq = ws_get("w_cp", 0, 8, f * 128, 128)
              b = banks(2)
              for k in range(8):
                  w = slq.sl(k, wcol)
                  mm(psr(b), w, CV.sl(k, slice(0, 512)), k == 0, k == 7)
                  mm(psr(b + 1), w, CV.sl(k, slice(512, 1024)), k == 0, k == 7)
              ws_done()
              tt("dve", SGC.sl(), psr2(b), SGC.sl(), ALU.mult)
              tt("dve", MG.sl(f), M1.sl(), SGC.sl(), ALU.add)

          chk("E", ps_)
          for t in range(4, NT):
              dma("sp", Hb.full[:, t, :], xin[ps_, 128 + t * 128:256 + t * 128, :], writes=Hb.sl(t).toks, key=("x", t))
          load_g(1)
          for nb in range(4):
              for kh in range(2):
                  slab = ws_get("w_mix", kh * 8, 8, nb * 512, 512)
                  for t in range(NT):
                      b = banks(1)
                      for k in range(8):
                          mm(psr(b), MG.sl(kh * 8 + k, slice(t * 128, t * 128 + 128)), slab.sl(k), k == 0, k == 7)
                      hs = Hb.sl(t, slice(nb * 512, nb * 512 + 512))
                      tt("dve", hs, psr(b), hs, ALU.add)
                  ws_done()

          chk("F", ps_)
          norm_phase([Hb.sl(t) for t in range(NT)], [128 + 128 * t for t in range(NT)])
          for s in range(2):
              slab = ws_get("w_xq", 0, 16, s * 256, 256)
              for hh in range(2):
                  hx = 2 * s + hh
                  b = banks(2)
                  for k in range(16):
                      w = slab.sl(k, slice(hh * 128, hh * 128 + 128))
                      mm(psr(b), w, UT.sl(k, slice(128, 640)), k == 0, k == 15)
                      mm(psr(b + 1), w, UT.sl(k, slice(640, 1152)), k == 0, k == 15)
                  cp("act", XQ.sl(hx), psr2(b))
              ws_done()
          PX = [Buf(sb, O_SCR + i * 2048, BF16, [2, 512]) for i in range(2)]
          RX = [Buf(sb, O_SCR + 4096 + i * 2048, F32, [512]) for i in range(2)]
          it = 0
          for hx in range(4):
              for th in range(2):
                  i2 = it % 2
                  it += 1
                  b = banks(2)
                  for mt in range(2):
                      mm(psr(b + mt), MEMK.sl(hx, slice(mt * 128, mt * 128 + 128)), XQ.sl(hx, slice(th * 512, th * 512 + 512)), True, True)
                  act(PX[i2].sl(), Ref(ps[:, b:b + 2, :], [("ps", b), ("ps", b + 1)]), AF.Exp, scale=float(128 ** -0.5))
                  bo = banks(2)
                  for mt in range(2):
                      mm(psr(bo), MEMV.sl(mt, slice(hx * 128, hx * 128 + 128)), PX[i2].sl(mt), mt == 0, mt == 1)
                  for mt in range(2):
                      mm(psr(bo + 1), ONES.sl(), PX[i2].sl(mt), mt == 0, mt == 1)
                  act(RX[i2].sl(), psr(bo + 1), AF.Ln)
                  act(RX[i2].sl(), RX[i2].sl(), AF.Exp, scale=-1.0)
                  tt("dve", XO.sl(hx, slice(th * 512, th * 512 + 512)), psr(bo), RX[i2].sl(), ALU.mult)
          load_g(3)
          xslabs = [ws_get("w_xo", 0, 4, nb * 512, 512) for nb in range(4)]
          for t in range(NT):
              for nb in range(4):
                  b = banks(1)
                  for k in range(4):
                      mm(psr(b), XO.sl(k, slice(t * 128, t * 128 + 128)), xslabs[nb].sl(k), k == 0, k == 3)
                  hs = Hb.sl(t, slice(nb * 512, nb * 512 + 512))
                  tt("dve", hs, psr(b), hs, ALU.add)
          ws_done(4)

          chk("G", ps_)
          norm_phase([Hb.sl(t) for t in range(NT)], [128 + 128 * t for t in range(NT)])
          SG = [Buf(sb, O_SCR + i * 4096, F32, [1024]) for i in range(2)]
          groups = [8, 8, 8, 8, 8, 4]
          j0 = 0
          it = 0
          for gi, G in enumerate(groups):
              AB = ACTb[gi % 2]
              for jl in range(G):
                  jj = j0 + jl
                  wcol = slice(0, 128)
                  i2 = it % 2
                  it += 1
                  slg = ws_get("w_f1", 0, 16, jj * 128, 128)
                  b = banks(2)
                  for k in range(16):
                      w = slg.sl(k, wcol)
                      mm(psr(b), w, UT.sl(k, slice(128, 640)), k == 0, k == 15)
                      mm(psr(b + 1), w, UT.sl(k, slice(640, 1152)), k == 0, k == 15)
                  ws_done()
                  act(SG[i2].sl(), psr2(b), AF.Silu)
                  slu = ws_get("w_f1", 0, 16, FFN + jj * 128, 128)
                  b = banks(2)
                  for k in range(16):
                      w = slu.sl(k, wcol)
                      mm(psr(b), w, UT.sl(k, slice(128, 640)), k == 0, k == 15)
                      mm(psr(b + 1), w, UT.sl(k, slice(640, 1152)), k == 0, k == 15)
                  ws_done()
                  tt("dve", AB.sl(jl), psr2(b), SG[i2].sl(), ALU.mult)
              if gi < len(groups) - 1:
                  for nb in range(4):
                      slab = ws_get("w_f2", j0, G, nb * 512, 512)
                      for t in range(NT):
                          b = banks(1)
                          for k in range(G):
                              mm(psr(b), AB.sl(k, slice(t * 128, t * 128 + 128)), slab.sl(k), k == 0, k == G - 1)
                          hs = Hb.sl(t, slice(nb * 512, nb * 512 + 512))
                          tt("dve", hs, psr(b), hs, ALU.add)
                      ws_done()
              else:
                  fslabs = [ws_get("w_f2", j0, G, nb * 512, 512) for nb in range(4)]
                  for t in range(NT):
                      for nb in range(4):
                          b = banks(1)
                          for k in range(G):
                              mm(psr(b), AB.sl(k, slice(t * 128, t * 128 + 128)), fslabs[nb].sl(k), k == 0, k == G - 1)
                          hs = Hb.sl(t, slice(nb * 512, nb * 512 + 512))
                          tt("dve", hs, psr(b), hs, ALU.add)
                  ws_done(4)
              j0 += G

          chk("H", ps_)
          load_g(4)
          for t in range(NT):
              rsc = norm_rstd1(Hb.sl(t))
              stt("dve", Hb.sl(t), Hb.sl(t), rsc, GB.sl(), ALU.mult, ALU.mult)
              key = ("o", t)
              if key not in out_keys:
                  out_keys.append(key)
              r0 = ps_ * T + t * 128
              dma("sp", yout[r0:r0 + 128, :], Hb.full[:, t, :], reads=Hb.sl(t).toks, key=key)
    except _Stop:
        pass

    if seq_in is not None:
        P.emit(nc, final_wait_keys=out_keys)
    st.close()
    nc._memmap = memmap
    return nc, seq_rec


_CACHE = {}


def _get_nc():
    if "nc" not in _CACHE:
        _, seq = _build(None)
        nc, _ = _build(seq)
        _CACHE["nc"] = nc
    return _CACHE["nc"]


def _host_inputs(x, mem, g_mix, w_in, conv_w, attn_sinks, w_attn_proj, w_conv_proj, w_mix_out,
                 g_xattn, g_mem, w_xq, w_xkv, w_xo, g_ffn, w_ffn_in, w_ffn_out, g_final):
    f32 = np.float32
    x = np.asarray(x, f32)
    mem = np.asarray(mem, f32)
    perm = []
    for c in range(8):
        for r in range(2):
            hq = (2 * (c // 4) + r) * 4 + (c % 4)
            perm.extend(range(hq * 64, hq * 64 + 64))
    perm = np.asarray(perm)
    w_in0 = np.asarray(w_in, f32)[0]
    w_in_p = np.ascontiguousarray(np.concatenate([w_in0[:, perm], w_in0[:, 1024:]], axis=1))
    w_ap_p = np.ascontiguousarray(np.asarray(w_attn_proj, f32)[0][perm, :])
    gvec = np.ascontiguousarray(np.stack([np.asarray(g_mix, f32)[0], np.asarray(g_xattn, f32)[0],
                                          np.asarray(g_mem, f32)[0], np.asarray(g_ffn, f32)[0],
                                          np.asarray(g_final, f32)]))
    cw = np.asarray(conv_w, f32)[0]
    cwin = np.ascontiguousarray(cw.reshape(3, 8, 128).transpose(2, 1, 0).reshape(128, 24))
    sinkin = np.ascontiguousarray(np.asarray(attn_sinks, f32)[0].reshape(1, 16))
    ident = np.eye(128, dtype=f32).astype(ml_dtypes.bfloat16)
    jj = np.arange(128)[:, None]
    ii = np.arange(128)[None, :]
    gen = np.stack([(jj > ii), (jj <= ii)]).astype(f32)
    genb = ((gen - 1.0) * 30000.0).transpose(2, 0, 1)
    half = 32
    inv_freq = (f32(10000.0) ** (-np.arange(half, dtype=f32) / f32(half))).astype(f32)
    shared = {"gvec": gvec, "w_in": w_in_p, "w_ap": w_ap_p,
              "w_cp": np.ascontiguousarray(np.asarray(w_conv_proj, f32)[0]),
              "w_mix": np.ascontiguousarray(np.asarray(w_mix_out, f32)[0]),
              "w_xq": np.ascontiguousarray(np.asarray(w_xq, f32)[0]),
              "w_xkv": np.ascontiguousarray(np.asarray(w_xkv, f32)[0]),
              "w_xo": np.ascontiguousarray(np.asarray(w_xo, f32)[0]),
              "w_f1": np.ascontiguousarray(np.asarray(w_ffn_in, f32)[0]),
              "w_f2": np.ascontiguousarray(np.asarray(w_ffn_out, f32)[0]),
              "cwin": cwin, "sinkin": sinkin, "identin": ident}
    maps = []
    for c in range(NCORES):
        b = c // 2
        xin = np.zeros((NPASS, TH, D), f32)
        rope = np.zeros((NPASS, 128, 2, 9, 32), f32)
        masks = np.zeros((128, 3, 2, 128), f32)
        masks[:, 0] = genb
        for p in range(NPASS):
            start = (c % 2) * TOK_CORE + p * T
            if start > 0:
                xin[p, 0:128] = x[b, start - 128:start]
            xin[p, 128:] = x[b, start:start + T]
            pos = (start - 128 + np.arange(9 * 128)).astype(f32)
            ang = (pos[:, None] * inv_freq[None, :]).astype(f32)
            cs = np.cos(ang).astype(f32).reshape(9, 128, 32)
            sn = np.sin(ang).astype(f32).reshape(9, 128, 32)
            rope[p, :, 0] = cs.transpose(1, 0, 2)
            rope[p, :, 1] = sn.transpose(1, 0, 2)
            m = genb.copy()
            if start == 0:
                m[:, 0, :] = -30000.0
            masks[:, 1 + p] = m
        d = dict(shared)
        d["xin"] = xin
        d["memin"] = np.ascontiguousarray(mem[b])
        d["ropein"] = rope
        d["maskin"] = masks.astype(ml_dtypes.bfloat16)
        maps.append(d)
    return maps


def kernel(**inputs):
    maps = _host_inputs(**inputs)
    nc = _get_nc()
    res = run_bass_kernel_spmd(nc, maps, core_ids=list(range(NCORES)))
    out = np.empty((BATCH, SEQ, D), np.float32)
    for c in range(NCORES):
        b = c // 2
        s0 = (c % 2) * TOK_CORE
        out[b, s0:s0 + TOK_CORE] = res.results[c]["yout"]
    return out
```

```python
import numpy as np
import ml_dtypes
import concourse.bass as bass
import concourse.mybir as mybir
from concourse.bass_utils import run_bass_kernel_spmd

F32 = mybir.dt.float32
BF16 = mybir.dt.bfloat16
U8 = mybir.dt.uint8
AF = mybir.ActivationFunctionType
ALU = mybir.AluOpType

UNIT = 512


class _Op:
    __slots__ = ("eng", "fn", "deps_c", "deps_d", "dma", "sig", "seq", "idx")


class Prog:
    ENGS = ("pe", "act", "dve", "pool", "sp")

    def __init__(self):
        self.ops = []
        self.last_w = {}
        self.readers = {}
        self.dma_count = {}

    def op(self, eng, fn, reads=(), writes=(), dma=None):
        o = _Op()
        o.eng, o.fn, o.idx = eng, fn, len(self.ops)
        o.dma = None
        o.sig = False
        o.seq = 0
        deps = set()
        lw, rd = self.last_w, self.readers
        for t in reads:
            w = lw.get(t)
            if w is not None:
                deps.add(w)
        for t in writes:
            w = lw.get(t)
            if w is not None:
                deps.add(w)
            r = rd.get(t)
            if r:
                deps.update(r.values())
        for t in writes:
            lw[t] = o.idx
            rd[t] = {}
        for t in reads:
            r = rd.get(t)
            if r is None:
                r = rd[t] = {}
            r[eng if dma is None else ("dma", o.idx)] = o.idx
        deps.discard(o.idx)
        dc, dd = {}, {}
        for d in deps:
            p = self.ops[d]
            if p.dma is not None:
                k, c = p.dma
                if dd.get(k, 0) < c:
                    dd[k] = c
            else:
                if p.eng == "pe" and eng == "pe" and dma is None:
                    continue
                if dc.get(p.eng, -1) < d:
                    dc[p.eng] = d
        o.deps_c, o.deps_d = dc, dd
        if dma is not None:
            c = self.dma_count.get(dma, 0) + 1
            self.dma_count[dma] = c
            o.dma = (dma, 16 * c)
        self.ops.append(o)
        return o

    def emit(self, nc, final_wait_keys=()):
        ops = self.ops
        for o in ops:
            for d in o.deps_c.values():
                ops[d].sig = True
        cnt = {e: 0 for e in self.ENGS}
        for o in ops:
            if o.dma is None and o.sig:
                cnt[o.eng] += 1
                o.seq = cnt[o.eng]
        from contextlib import ExitStack
        with ExitStack() as st:
            esem = {e: st.enter_context(nc.semaphore("s_" + e)) for e in self.ENGS}
            dsem = {k: st.enter_context(nc.semaphore("d_%d" % i)) for i, k in enumerate(self.dma_count)}
            block = st.enter_context(nc.Block())
            per = {e: [o for o in ops if o.eng == e] for e in self.ENGS}

            def body(ename):
                def run(e):
                    wc = {x: 0 for x in self.ENGS}
                    wd = {}
                    for o in per[ename]:
                        for pe_, d in o.deps_c.items():
                            need = ops[d].seq
                            if wc[pe_] < need:
                                e.wait_ge(esem[pe_], need)
                                wc[pe_] = need
                        for k, c in o.deps_d.items():
                            if wd.get(k, 0) < c:
                                e.wait_ge(dsem[k], c)
                                wd[k] = c
                        ins = o.fn(e)
                        if o.dma is not None:
                            ins.then_inc(dsem[o.dma[0]], 16)
                        elif o.sig:
                            ins.then_inc(esem[ename], 1)
                    if ename == "sp":
                        for k in final_wait_keys:
                            e.wait_ge(dsem[k], 16 * self.dma_count[k])
                return run

            block.tensor(body("pe"))
            block.scalar(body("act"))
            block.vector(body("dve"))
            block.gpsimd(body("pool"))
            block.sync(body("sp"))


D = 2048
SEQ = 4096
BATCH = 4
NCORES = 8
TOK_CORE = 2048
NPASS = 2
T = 1024
NT = 8
TH = T + 128
FFN = 5632
EPS = 1e-6
NSLOT = 4
SLOT_BYTES = 8192


class Ref:
    __slots__ = ("ap", "toks")

    def __init__(self, ap, toks):
        self.ap, self.toks = ap, toks


class Buf:
    def __init__(self, sb, off, dt, shape):
        self.sb, self.off, self.dt, self.shape = sb, off, dt, tuple(shape)
        self.esz = 2 if dt == BF16 else 4
        n = 1
        for s in shape:
            n *= s
        self.n = n
        self.nbytes = n * self.esz
        full = sb[:, off:off + self.nbytes].bitcast(dt)
        if len(shape) == 2:
            full = full.rearrange("p (a b) -> p a b", a=shape[0])
        elif len(shape) == 3:
            full = full.rearrange("p (a b c) -> p a b c", a=shape[0], b=shape[1])
        self.full = full

    def sl(self, *idx, p=(0, 128)):
        shape = self.shape
        idx = list(idx) + [slice(None)] * (len(shape) - len(idx))
        rng = []
        for i, s in zip(idx, shape):
            if isinstance(i, int):
                rng.append((i, i + 1))
            else:
                a = 0 if i.start is None else i.start
                b = s if i.stop is None else i.stop
                rng.append((a, b))
        ap = self.full[(slice(p[0], p[1]),) + tuple(idx)]
        strides = [1] * len(shape)
        for d in range(len(shape) - 2, -1, -1):
            strides[d] = strides[d + 1] * shape[d + 1]
        toks = set()
        lead = rng[:-1]
        la, lb = rng[-1]

        def rec(d, base):
            if d == len(lead):
                b0 = self.off + (base + la) * self.esz
                b1 = self.off + (base + lb) * self.esz - 1
                for u in range(b0 // UNIT, b1 // UNIT + 1):
                    toks.add(u)
                return
            for i in range(lead[d][0], lead[d][1]):
                rec(d + 1, base + i * strides[d])

        rec(0, 0)
        return Ref(ap, list(toks))


class _Stop(Exception):
    pass


def _build(seq_in=None, stop=None):
    nc = bass.Bass("TRN2", target_bir_lowering=False)
    P = Prog()
    dram = {}

    def din(name, shape, dt=F32):
        dram[name] = nc.dram_tensor(name, list(shape), dt, kind="ExternalInput").ap()
        return dram[name]

    xin = din("xin", [NPASS, TH, D])
    memin = din("memin", [256, D])
    gvec = din("gvec", [5, D])
    w_in = din("w_in", [D, 8704])
    w_ap = din("w_ap", [1024, D])
    w_cp = din("w_cp", [1024, D])
    w_mix = din("w_mix", [D, D])
    w_xq = din("w_xq", [D, 512])
    w_xkv = din("w_xkv", [D, 1024])
    w_xo = din("w_xo", [512, D])
    w_f1 = din("w_f1", [D, 2 * FFN])
    w_f2 = din("w_f2", [FFN, D])
    cwin = din("cwin", [128, 24])
    sinkin = din("sinkin", [1, 16])
    ropein = din("ropein", [NPASS, 128, 2, 9, 32])
    maskin = din("maskin", [128, 3, 2, 128], BF16)
    identin = din("identin", [128, 128], BF16)
    yout = nc.dram_tensor("yout", [TOK_CORE, D], F32, kind="ExternalOutput").ap()
    dbg = nc.dram_tensor("dbg", [128, 206 * 1024], U8, kind="ExternalOutput").ap() if stop else None
    memmap = {}
    wd = {"w_in": w_in, "w_ap": w_ap, "w_cp": w_cp, "w_mix": w_mix, "w_xq": w_xq, "w_xkv": w_xkv,
          "w_xo": w_xo, "w_f1": w_f1, "w_f2": w_f2}
    wview = {k: v.rearrange("(k p) c -> p k c", p=128) for k, v in wd.items()}

    from contextlib import ExitStack
    st = ExitStack()
    SBYTES = 206 * 1024
    sb = st.enter_context(nc.sbuf_tensor("sb", [128, SBYTES], U8))
    ps = st.enter_context(nc.psum_tensor("ps", [128, 8, 512], F32))

    off = [0]

    def alloc(nbytes, align=64):
        o = (off[0] + align - 1) // align * align
        off[0] = o + nbytes
        assert off[0] <= SBYTES, ("SBUF overflow", off[0])
        return o

    O_H = alloc(NT * 8192, 512)
    O_UT = alloc(16 * TH * 2, 512)
    O_R3 = alloc(32768, 512)
    O_WS = alloc(NSLOT * SLOT_BYTES, 512)
    O_SCR = alloc(16384, 512)
    O_GB = alloc(8192, 512)
    O_UB = alloc(4096, 512)
    O_ROPE = alloc(2 * 9 * 32 * 4, 512)
    O_MASK = alloc(3 * 2 * 128 * 2, 512)
    O_ID = alloc(256, 512)
    O_ONES = alloc(256, 512)
    O_MEMK = alloc(4 * 256 * 2, 512)
    O_MEMV = alloc(2 * 512 * 2, 512)
    O_SS = alloc(80 * 4, 512)
    O_RS = alloc(80 * 4, 512)
    O_CW = alloc(24 * 4, 512)
    O_SINK = alloc(16 * 4, 64)
    O_SK2 = alloc(2 * 4 * 4, 64)
    O_I4 = alloc(1024, 512)
    O_ONESP = alloc(512, 512)

    Hb = Buf(sb, O_H, F32, [NT, D])
    UT = Buf(sb, O_UT, BF16, [16, TH])
    GB = Buf(sb, O_GB, F32, [D])
    UB = Buf(sb, O_UB, BF16, [D])
    ROPE = Buf(sb, O_ROPE, F32, [2, 9, 32])
    MASK = Buf(sb, O_MASK, BF16, [3, 2, 128])
    IDT = Buf(sb, O_ID, BF16, [128])
    ONES = Buf(sb, O_ONES, BF16, [128])
    MEMK = Buf(sb, O_MEMK, BF16, [4, 256])
    MEMV = Buf(sb, O_MEMV, BF16, [2, 512])
    SS = Buf(sb, O_SS, F32, [80])
    RS = Buf(sb, O_RS, F32, [80])
    CW = Buf(sb, O_CW, F32, [24])
    SINK = Buf(sb, O_SINK, F32, [16])
    SK2 = Buf(sb, O_SK2, F32, [2, 4])
    I4 = Buf(sb, O_I4, BF16, [512])
    ONESP = Buf(sb, O_ONESP, BF16, [2, 128])
    AO = Buf(sb, O_H + 4 * 8192, BF16, [8, T])
    CV = Buf(sb, O_H + 6 * 8192, BF16, [8, T])
    QT = Buf(sb, O_R3, BF16, [8, T])
    KT = Buf(sb, O_R3 + 16384, BF16, [2, TH])
    VA = Buf(sb, O_R3 + 16384 + 4608, BF16, [9, 4, 128])
    MG = Buf(sb, O_R3, BF16, [16, T])
    XQ = Buf(sb, O_R3, BF16, [4, T])
    XO = Buf(sb, O_R3 + 8192, BF16, [4, T])
    ACTb = [Buf(sb, O_R3 + i * 16384, BF16, [8, T]) for i in range(2)]
    XH = Buf(sb, O_SCR, F32, [D])
    JUNK = Buf(sb, O_SCR + 8192, BF16, [D])

    def psr(b, c0=0, c1=512, p=(0, 128)):
        return Ref(ps[p[0]:p[1], b, c0:c1], [("ps", b)])

    def psr2(b, p=(0, 128)):
        return Ref(ps[p[0]:p[1], b:b + 2, :].rearrange("p a b -> p (a b)"), [("ps", b), ("ps", b + 1)])

    def psbf(b, c0, c1):
        return Ref(ps[:, b, :].bitcast(BF16)[:, c0:c1], [("ps", b)])

    bank_ctr = [0]

    def banks(n=1):
        b = bank_ctr[0]
        if n == 2 and b % 2:
            b += 1
        if b + n > 8:
            b = 0
        bank_ctr[0] = (b + n) % 8
        return b

    def mm(out, lhsT, rhs, start, stop):
        P.op("pe", lambda e: e.matmul(out.ap, lhsT=lhsT.ap, rhs=rhs.ap, start=start, stop=stop),
             reads=lhsT.toks + rhs.toks, writes=out.toks)

    def tr(out, in_):
        P.op("pe", lambda e: e.transpose(out=out.ap, in_=in_.ap, identity=IDT.full),
             reads=in_.toks + IDT.sl().toks, writes=out.toks)

    def act(out, in_, func, scale=1.0, bias=0.0, accum=None):
        sc = scale.ap if isinstance(scale, Ref) else scale
        rd = in_.toks + (scale.toks if isinstance(scale, Ref) else [])
        wr = out.toks + (accum.toks if accum is not None else [])
        if accum is None:
            P.op("act", lambda e: e.activation(out=out.ap, in_=in_.ap, func=func, bias=bias, scale=sc),
                 reads=rd, writes=wr)
        else:
            P.op("act", lambda e: e.activation(out=out.ap, in_=in_.ap, func=func, bias=bias, scale=sc,
                                               accum_out=accum.ap), reads=rd, writes=wr)

    def cp(eng, out, in_):
        if eng == "act":
            P.op("act", lambda e: e.copy(out=out.ap, in_=in_.ap), reads=in_.toks, writes=out.toks)
        else:
            P.op(eng, lambda e: e.tensor_copy(out=out.ap, in_=in_.ap), reads=in_.toks, writes=out.toks)

    def tt(eng, out, in0, in1, op):
        P.op(eng, lambda e: e.tensor_tensor(out=out.ap, in0=in0.ap, in1=in1.ap, op=op),
             reads=in0.toks + in1.toks, writes=out.toks)

    def ts(eng, out, in0, s1, s2, op0, op1=None):
        a1 = s1.ap if isinstance(s1, Ref) else s1
        a2 = s2.ap if isinstance(s2, Ref) else s2
        rd = in0.toks + (s1.toks if isinstance(s1, Ref) else []) + (s2.toks if isinstance(s2, Ref) else [])
        if op1 is None:
            P.op(eng, lambda e: e.tensor_scalar(out=out.ap, in0=in0.ap, scalar1=a1, scalar2=None, op0=op0),
                 reads=rd, writes=out.toks)
        else:
            P.op(eng, lambda e: e.tensor_scalar(out=out.ap, in0=in0.ap, scalar1=a1, scalar2=a2, op0=op0, op1=op1),
                 reads=rd, writes=out.toks)

    def stt(eng, out, in0, scalar, in1, op0, op1):
        a = scalar.ap if isinstance(scalar, Ref) else scalar
        rd = in0.toks + in1.toks + (scalar.toks if isinstance(scalar, Ref) else [])
        P.op(eng, lambda e: e.scalar_tensor_tensor(out=out.ap, in0=in0.ap, scalar=a, in1=in1.ap, op0=op0, op1=op1),
             reads=rd, writes=out.toks)

    def memset(eng, out, val):
        P.op(eng, lambda e: e.memset(out.ap, val), writes=out.toks)

    dma_ctr = [0]

    def dma(eng, out_ap, in_ap, reads=(), writes=(), key=None):
        if key is None:
            dma_ctr[0] += 1
            key = ("m", dma_ctr[0] % 6)
        P.op(eng, lambda e: e.dma_start(out=out_ap, in_=in_ap), reads=list(reads), writes=list(writes), dma=key)

    RING = NSLOT * SLOT_BYTES
    seq_rec = []
    ws = {"get": 0, "emit": 0, "rel": 0}
    ws_offs, ws_need = [], []
    if seq_in is not None:
        head = 0
        for (_n, _k0, nk, _c0, ncols) in seq_in:
            size = nk * ncols * 2
            o = (head + size - 1) // size * size
            if o + size > RING:
                o = 0
            ws_offs.append(o)
            head = o + size
        for j, dj in enumerate(seq_in):
            sj = dj[2] * dj[4] * 2
            nd = -1
            for i in range(j - 1, max(-1, j - 40), -1):
                si = seq_in[i][2] * seq_in[i][4] * 2
                if ws_offs[i] < ws_offs[j] + sj and ws_offs[j] < ws_offs[i] + si:
                    nd = i
                    break
            ws_need.append(nd)

    def ws_dma(name, k0, nk, c0, ncols, o):
        b = Buf(sb, O_WS + o, BF16, [nk, ncols])
        r = b.sl()
        src = wview[name][:, k0:k0 + nk, c0:c0 + ncols]
        P.op("pool", lambda e: e.dma_start(out=r.ap, in_=src), writes=r.toks, dma=("ws", o // 2048))
        return b

    def ws_pump():
        while ws["emit"] < len(seq_in) and ws_need[ws["emit"]] < ws["rel"]:
            j = ws["emit"]
            ws_dma(*seq_in[j], ws_offs[j])
            ws["emit"] = j + 1

    def ws_get(name, k0, nk, c0, ncols):
        assert nk * ncols * 2 <= SLOT_BYTES
        d = (name, k0, nk, c0, ncols)
        seq_rec.append(d)
        j = ws["get"]
        ws["get"] = j + 1
        if seq_in is None:
            return ws_dma(*d, 0)
        assert seq_in[j] == d, (j, seq_in[j], d)
        ws_pump()
        assert ws["emit"] > j, ("weight ring too small for slabs held at once", j)
        return Buf(sb, O_WS + ws_offs[j], BF16, [nk, ncols])

    def ws_done(n=1):
        ws["rel"] += n
        if seq_in is not None:
            ws_pump()

    dma("sp", IDT.full, identin, writes=IDT.sl().toks, key="c0")
    dma("sp", MASK.full, maskin, writes=MASK.sl().toks, key="c1")
    dma("sp", CW.full, cwin, writes=CW.sl().toks, key="c2")
    dma("sp", SINK.full, sinkin[0:1, :].broadcast_to([128, 16]), writes=SINK.sl().toks, key="c3")
    memset("dve", ONES.sl(), 1.0)
    memset("dve", SS.sl(), 0.0)
    act(SINK.sl(), SINK.sl(), AF.Exp)
    for kc_ in range(2):
        for hh_ in range(2):
            pr_ = (64 * hh_, 64 * hh_ + 64)
            hk_ = 2 * kc_ + hh_
            cp("dve", SK2.sl(kc_, p=pr_), SINK.sl(slice(hk_ * 4, hk_ * 4 + 4), p=pr_))
    for g_ in range(4):
        cp("dve", I4.sl(slice(g_ * 128, g_ * 128 + 128)), IDT.sl())
    memset("dve", ONESP.sl(), 0.0)
    memset("dve", ONESP.sl(0, slice(0, 64)), 1.0)
    memset("dve", ONESP.sl(1, slice(64, 128)), 1.0)
    stat_ctr = [0]

    def load_g(i):
        dma("sp", GB.full, gvec[i:i + 1, :].broadcast_to([128, D]), writes=GB.sl().toks, key="g")

    UB2 = Buf(sb, O_SCR + 12288, BF16, [D])

    def norm_rstd(xts):
        n = len(xts)
        c0 = stat_ctr[0]
        stat_ctr[0] += n
        for i, xt in enumerate(xts):
            act(JUNK.sl(), xt, AF.Square, accum=SS.sl(slice(c0 + i, c0 + i + 1)))
        blk = RS.sl(slice(c0, c0 + n))
        ts("dve", blk, SS.sl(slice(c0, c0 + n)), 1.0 / D, EPS, ALU.mult, ALU.add)
        act(blk, blk, AF.Sqrt)
        P.op("dve", lambda e: e.reciprocal(out=blk.ap, in_=blk.ap), reads=blk.toks, writes=blk.toks)
        return [RS.sl(slice(c0 + i, c0 + i + 1)) for i in range(n)]

    def norm_rstd1(xt):
        c = stat_ctr[0]
        stat_ctr[0] += 1
        ssc = SS.sl(slice(c, c + 1))
        rsc = RS.sl(slice(c, c + 1))
        act(JUNK.sl(), xt, AF.Square, accum=ssc)
        ts("dve", rsc, ssc, 1.0 / D, EPS, ALU.mult, ALU.add)
        act(rsc, rsc, AF.Sqrt)
        P.op("dve", lambda e: e.reciprocal(out=rsc.ap, in_=rsc.ap), reads=rsc.toks, writes=rsc.toks)
        return rsc

    def norm_phase(xts, cols, dst=None):
        dst = UT if dst is None else dst
        rs = norm_rstd(xts)
        for i, (xt, col0) in enumerate(zip(xts, cols)):
            ub = UB if i % 2 == 0 else UB2
            stt("dve", ub.sl(), xt, rs[i], GB.sl(), ALU.mult, ALU.mult)
            for half in range(2):
                b = banks(1)
                for j in range(8):
                    k = half * 8 + j
                    tr(psbf(b, j * 128, (j + 1) * 128), ub.sl(slice(k * 128, (k + 1) * 128)))
                src = Ref(ps[:, b, :].bitcast(BF16).rearrange("p (a b) -> p a b", a=8), [("ps", b)])
                cp("act", dst.sl(slice(half * 8, half * 8 + 8), slice(col0, col0 + 128)), src)

    out_keys = []

    def chk(name, ps_):
        if stop == (name, ps_):
            allt = list(range(0, SBYTES // UNIT))
            dma("sp", dbg, sb[:, :], reads=allt, key="dbg")
            out_keys.append("dbg")
            raise _Stop()

    memmap.update(dict(O_H=O_H, O_UT=O_UT, O_R3=O_R3, O_SCR=O_SCR, O_MEMK=O_MEMK, O_MEMV=O_MEMV, O_RS=O_RS, O_SS=O_SS,
                       O_SINK=O_SINK, O_GB=O_GB, O_UB=O_UB, O_ROPE=O_ROPE, O_MASK=O_MASK))
    try:
      for ps_ in range(NPASS):
          dma("sp", ROPE.full, ropein[ps_], writes=ROPE.sl().toks, key="rope")
          load_g(0)
          dma("sp", XH.full, xin[ps_, 0:128, :], writes=XH.sl().toks, key="xh")
          for t in range(NT):
              dma("sp", Hb.full[:, t, :], xin[ps_, 128 + t * 128:256 + t * 128, :], writes=Hb.sl(t).toks, key=("x", t))
          norm_phase([XH.sl()] + [Hb.sl(t) for t in range(NT)], [128 * t for t in range(NT + 1)])

          chk("A", ps_)
          memset("dve", VA.sl(), 0.0)
          SC_A = Buf(sb, O_SCR, F32, [2, 256])
          SC_B = Buf(sb, O_SCR + 2048, F32, [2, 256])
          SC_Q = Buf(sb, O_SCR + 4096, BF16, [2, 256])
          it = 0
          pend = []
          for s in range(6):
              slab = ws_get("w_in", 0, 16, s * 256, 256)
              for t in range(9):
                  if s < 4 and t == 0:
                      continue
                  b = banks(1)
                  for k in range(16):
                      mm(psr(b, 0, 256), UT.sl(k, slice(t * 128, t * 128 + 128)), slab.sl(k), k == 0, k == 15)
                  if s == 5:
                      pv = ps[:, b, 0:256].rearrange("p (a b c) -> p a b c", a=2, b=2)
                      va = VA.full[:, t, :, :].rearrange("p (a b) c -> p a b c", a=2)
                      P.op("act", lambda e, pv=pv, va=va: e.copy(out=va[:, :, 0, 0:64], in_=pv[:, :, 0, :]),
                           reads=[("ps", b)], writes=VA.sl(t).toks)
                      P.op("dve", lambda e, pv=pv, va=va: e.tensor_copy(out=va[:, :, 1, 64:128], in_=pv[:, :, 1, :]),
                           reads=[("ps", b)], writes=VA.sl(t).toks)
                      continue
                  i2 = it % 2
                  it += 1
                  A_ = SC_A.sl(i2)
                  B_ = SC_B.sl(i2)
                  Q_ = SC_Q.sl(i2)
                  pq = ps[:, b, 0:256].rearrange("p (h two f) -> p h two f", h=4, two=2)
                  cosb = ROPE.full[:, 0, t, :].unsqueeze(1).unsqueeze(1).broadcast_to([128, 4, 2, 32])
                  sinb = ROPE.full[:, 1, t, :].unsqueeze(1).broadcast_to([128, 4, 32])
                  Aap = SC_A.full[:, i2, :].rearrange("p (h two f) -> p h two f", h=4, two=2)
                  Bap = SC_B.full[:, i2, :].rearrange("p (h two f) -> p h two f", h=4, two=2)
                  rt = ROPE.sl().toks
                  P.op("dve", lambda e, Aap=Aap, pq=pq, cosb=cosb: e.tensor_tensor(out=Aap, in0=pq, in1=cosb, op=ALU.mult),
                       reads=[("ps", b)] + rt, writes=A_.toks)
                  P.op("dve", lambda e, Bap=Bap, pq=pq, sinb=sinb: e.scalar_tensor_tensor(
                      out=Bap[:, :, 0, :], in0=pq[:, :, 1, :], scalar=-1.0, in1=sinb, op0=ALU.mult, op1=ALU.mult),
                      reads=[("ps", b)] + rt, writes=B_.toks)
                  P.op("dve", lambda e, Bap=Bap, pq=pq, sinb=sinb: e.tensor_tensor(
                      out=Bap[:, :, 1, :], in0=pq[:, :, 0, :], in1=sinb, op=ALU.mult),
                      reads=[("ps", b)] + rt, writes=B_.toks)
                  tt("dve", Q_, A_, B_, ALU.add)

                  def fin(s=s, t=t, i2=i2):
                      b2 = banks(1)
                      for j in range(2):
                          tr(psbf(b2, j * 128, (j + 1) * 128), SC_Q.sl(i2, slice(j * 128, (j + 1) * 128)))
                      src = Ref(ps[:, b2, :].bitcast(BF16)[:, 0:256].rearrange("p (a b) -> p a b", a=2), [("ps", b2)])
                      if s < 4:
                          cp("act", QT.sl(slice(2 * s, 2 * s + 2), slice((t - 1) * 128, t * 128)), src)
                      else:
                          cp("act", KT.sl(slice(0, 2), slice(t * 128, (t + 1) * 128)), src)
                  pend.append(fin)
                  if len(pend) > 1:
                      pend.pop(0)()
              ws_done()
          while pend:
              pend.pop(0)()

          chk("B", ps_)
          PT = [Buf(sb, O_SCR + i * 4096, BF16, [2, 2, 512]) for i in range(2)]
          RC = [Buf(sb, O_SCR + 8192 + i * 2048, F32, [512]) for i in range(2)]
          iters = [(n, kc) for n in range(NT) for kc in range(2)]

          def qk(n, kc, i2):
            mi = (1 + ps_) if n == 0 else 0
            for hh in range(2):
                pr = (64 * hh, 64 * hh + 64)
                for j in range(2):
                    bk = 2 * hh + j
                    mm(psr(bk), KT.sl(kc, slice((n + j) * 128, (n + j + 1) * 128), p=pr),
                       QT.sl(slice(4 * kc, 4 * kc + 4), slice(n * 128, (n + 1) * 128), p=pr), True, False)
                    mm(psr(bk), MASK.sl(mi, j), I4.sl(), False, True)
                act(PT[i2].sl(hh), Ref(ps[:, 2 * hh:2 * hh + 2, :], [("ps", 2 * hh), ("ps", 2 * hh + 1)]), AF.Exp, scale=0.125)

          def pv(n, kc, i2):
            bo = 4 + 2 * i2
            for q_ in range(4):
                hh, j = q_ // 2, q_ % 2
                mm(psr(bo), VA.sl(n + j, 2 * kc + hh), PT[i2].sl(hh, j), q_ == 0, q_ == 3)
            for q_ in range(4):
                hh, j = q_ // 2, q_ % 2
                mm(psr(bo + 1), ONESP.sl(hh), PT[i2].sl(hh, j), q_ == 0, q_ == 3)
            skb = SK2.full[:, kc, :].unsqueeze(2).broadcast_to([128, 4, 128])
            rc = RC[i2].full.rearrange("p (g q) -> p g q", g=4)
            pd = ps[:, bo + 1, :].rearrange("p (g q) -> p g q", g=4)
            P.op("dve", lambda e: e.tensor_tensor(out=rc, in0=pd, in1=skb, op=ALU.add),
                 reads=[("ps", bo + 1)] + SK2.sl().toks, writes=RC[i2].sl().toks)
            act(RC[i2].sl(), RC[i2].sl(), AF.Ln)
            act(RC[i2].sl(), RC[i2].sl(), AF.Exp, scale=-1.0)
            po = ps[:, bo, :].rearrange("p (g q) -> p g q", g=4)
            ao = AO.sl(slice(4 * kc, 4 * kc + 4), slice(n * 128, (n + 1) * 128))
            P.op("dve", lambda e: e.tensor_tensor(out=ao.ap, in0=po, in1=rc, op=ALU.mult),
                 reads=[("ps", bo)] + RC[i2].sl().toks, writes=ao.toks)

          for i, (n, kc) in enumerate(iters):
              qk(n, kc, i % 2)
              if i > 0:
                  pv(*iters[i - 1], (i - 1) % 2)
          pv(*iters[-1], (len(iters) - 1) % 2)
          bank_ctr[0] = 0

          chk("C", ps_)
          ZS = Buf(sb, O_SCR, F32, [1026])
          CZ = Buf(sb, O_SCR + 4608, F32, [1026])
          AC = Buf(sb, O_SCR + 9216, F32, [1024])
          for c in range(8):
              wcol = slice(0, 128)
              slz = ws_get("w_in", 0, 16, 1536 + c * 128, 128)
              bz = banks(2)
              bzh = banks(1)
              for k in range(16):
                  w = slz.sl(k, wcol)
                  mm(psr(bz), w, UT.sl(k, slice(128, 640)), k == 0, k == 15)
                  mm(psr(bz + 1), w, UT.sl(k, slice(640, 1152)), k == 0, k == 15)
                  mm(psr(bzh, 0, 2), w, UT.sl(k, slice(126, 128)), k == 0, k == 15)
              ws_done()
              cp("act", ZS.sl(slice(2, 1026)), psr2(bz))
              cp("act", ZS.sl(slice(0, 2)), psr(bzh, 0, 2))
              slg = ws_get("w_in", 0, 16, 3584 + c * 128, 128)
              bg = banks(2)
              bgh = banks(1)
              for k in range(16):
                  w = slg.sl(k, wcol)
                  mm(psr(bg), w, UT.sl(k, slice(128, 640)), k == 0, k == 15)
                  mm(psr(bg + 1), w, UT.sl(k, slice(640, 1152)), k == 0, k == 15)
                  mm(psr(bgh, 0, 2), w, UT.sl(k, slice(126, 128)), k == 0, k == 15)
              ws_done()
              tt("dve", CZ.sl(slice(2, 1026)), psr2(bg), ZS.sl(slice(2, 1026)), ALU.mult)
              tt("dve", CZ.sl(slice(0, 2)), psr(bgh, 0, 2), ZS.sl(slice(0, 2)), ALU.mult)
              slb = ws_get("w_in", 0, 16, 2560 + c * 128, 128)
              bb = banks(2)
              for k in range(16):
                  w = slb.sl(k, wcol)
                  mm(psr(bb), w, UT.sl(k, slice(128, 640)), k == 0, k == 15)
                  mm(psr(bb + 1), w, UT.sl(k, slice(640, 1152)), k == 0, k == 15)
              ws_done()
              act(AC.sl(), CZ.sl(slice(0, 1024)), AF.Copy, scale=CW.sl(slice(c * 3, c * 3 + 1)))
              stt("dve", AC.sl(), CZ.sl(slice(1, 1025)), CW.sl(slice(c * 3 + 1, c * 3 + 2)), AC.sl(), ALU.mult, ALU.add)
              stt("dve", AC.sl(), CZ.sl(slice(2, 1026)), CW.sl(slice(c * 3 + 2, c * 3 + 3)), AC.sl(), ALU.mult, ALU.add)
              tt("dve", CV.sl(c), psr2(bb), AC.sl(), ALU.mult)

          chk("D", ps_)
          SGA = Buf(sb, O_SCR, F32, [1024])
          M1 = Buf(sb, O_SCR + 4096, F32, [1024])
          SGC = Buf(sb, O_SCR + 8192, F32, [1024])
          for f in range(16):
              wcol = slice(0, 128)
              sla = ws_get("w_in", 0, 16, 4608 + f * 128, 128)
              b = banks(2)
              for k in range(16):
                  w = sla.sl(k, wcol)
                  mm(psr(b), w, UT.sl(k, slice(128, 640)), k == 0, k == 15)
                  mm(psr(b + 1), w, UT.sl(k, slice(640, 1152)), k == 0, k == 15)
              ws_done()
              act(SGA.sl(), psr2(b), AF.Sigmoid)
              slp = ws_get("w_ap", 0, 8, f * 128, 128)
              b = banks(2)
              for k in range(8):
                  w = slp.sl(k, wcol)
                  mm(psr(b), w, AO.sl(k, slice(0, 512)), k == 0, k == 7)
                  mm(psr(b + 1), w, AO.sl(k, slice(512, 1024)), k == 0, k == 7)
              ws_done()
              tt("dve", M1.sl(), psr2(b), SGA.sl(), ALU.mult)
              slc = ws_get("w_in", 0, 16, 6656 + f * 128, 128)
              b = banks(2)
              for k in range(16):
                  w = slc.sl(k, wcol)
                  mm(psr(b), w, UT.sl(k, slice(128, 640)), k == 0, k == 15)
                  mm(psr(b + 1), w, UT.sl(k, slice(640, 1152)), k == 0, k == 15)
              ws_done()
              act(SGC.sl(), psr2(b), AF.Sigmoid)
              slq = ws_get("w_cp", 0, 8, f * 128, 128)
              b = banks(2)
              for k in range(8):
                  w = slq.sl(k, wcol)
                  mm(psr(b), w, CV.sl(k, slice(0, 512)), k == 0, k == 7)
                  mm(psr(b + 1), w, CV.sl(k, slice(512, 1024)), k == 0, k == 7)
              ws_done()
              tt("dve", SGC.sl(), psr2(b), SGC.sl(), ALU.mult)
              tt("dve", MG.sl(f), M1.sl(), SGC.sl(), ALU.add)

          chk("E", ps_)
          for t in range(4, NT):
              dma("sp", Hb.full[:, t, :], xin[ps_, 128 + t * 128:256 + t * 128, :], writes=Hb.sl(t).toks, key=("x", t))
          load_g(1)
          for nb in range(4):
              for kh in range(2):
                  slab = ws_get("w_mix", kh * 8, 8, nb * 512, 512)
                  for t in range(NT):
                      b = banks(1)
                      for k in range(8):
                          mm(psr(b), MG.sl(kh * 8 + k, slice(t * 128, t * 128 + 128)), slab.sl(k), k == 0, k == 7)
                      hs = Hb.sl(t, slice(nb * 512, nb * 512 + 512))
                      tt("dve", hs, psr(b), hs, ALU.add)
                  ws_done()

          chk("F", ps_)
          if ps_ == 0:
            MT = Buf(sb, O_R3 + 16384, BF16, [16, 256])
            load_g(2)
            for mt in range(2):
                dma("sp", XH.full, memin[mt * 128:(mt + 1) * 128, :], writes=XH.sl().toks, key="xh")
                norm_phase([XH.sl()], [mt * 128], dst=MT)
            for s in range(2):
                slab = ws_get("w_xkv", 0, 16, s * 256, 256)
                for hh in range(2):
                    hx = s * 2 + hh
                    b = banks(1)
                    for k in range(16):
                        mm(psr(b, 0, 256), slab.sl(k, slice(hh * 128, hh * 128 + 128)), MT.sl(k, slice(0, 256)), k == 0, k == 15)
                    cp("act", MEMK.sl(hx), psr(b, 0, 256))
                ws_done()
            for s in range(2):
                slab = ws_get("w_xkv", 0, 16, 512 + s * 256, 256)
                for mt in range(2):
                    b = banks(1)
                    for k in range(16):
                        mm(psr(b, 0, 256), MT.sl(k, slice(mt * 128, mt * 128 + 128)), slab.sl(k), k == 0, k == 15)
                    cp("dve", MEMV.sl(mt, slice(s * 256, s * 256 + 256)), psr(b, 0, 256))
                ws_done()

          load_g(1)

          norm_phase([Hb.sl(t) for t in range(NT)], [128 + 128 * t for t in range(NT)])
          for s in range(2):
              slab = ws_get("w_xq", 0, 16, s * 256, 256)
              for hh in range(2):
                  hx = 2 * s + hh
                  b = banks(2)
                  for k in range(16):
                      w = slab.sl(k, slice(hh * 128, hh * 128 + 128))
                      mm(psr(b), w, UT.sl(k, slice(128, 640)), k == 0, k == 15)
                      mm(psr(b + 1), w, UT.sl(k, slice(640, 1152)), k == 0, k == 15)
                  cp("act", XQ.sl(hx), psr2(b))
              ws_done()
          PX = [Buf(sb, O_SCR + i * 2048, BF16, [2, 512]) for i in range(2)]
          RX = [Buf(sb, O_SCR + 4096 + i * 2048, F32, [512]) for i in range(2)]
          it = 0
          for hx in range(4):
              for th in range(2):
                  i2 = it % 2
                  it += 1
                  b = banks(2)
                  for mt in range(2):
                      mm(psr(b + mt), MEMK.sl(hx, slice(mt * 128, mt * 128 + 128)), XQ.sl(hx, slice(th * 512, th * 512 + 512)), True, True)
                  act(PX[i2].sl(), Ref(ps[:, b:b + 2, :], [("ps", b), ("ps", b + 1)]), AF.Exp, scale=float(128 ** -0.5))
                  bo = banks(2)
                  for mt in range(2):
                      mm(psr(bo), MEMV.sl(mt, slice(hx * 128, hx * 128 + 128)), PX[i2].sl(mt), mt == 0, mt == 1)
                  for mt in range(2):
                      mm(psr(bo + 1), ONES.sl(), PX[i2].sl(mt), mt == 0, mt == 1)
                  act(RX[i2].sl(), psr(bo + 1), AF.Ln)
                  act(RX[i2].sl(), RX[i2].sl(), AF.Exp, scale=-1.0)
                  tt("dve", XO.sl(hx, slice(th * 512, th * 512 + 512)), psr(bo), RX[i2].sl(), ALU.mult)
          load_g(3)
          xslabs = [ws_get("w_xo", 0, 4, nb * 512, 512) for nb in range(4)]
          for t in range(NT):
              for nb in range(4):
                  b = banks(1)
                  for k in range(4):
                      mm(psr(b), XO.sl(k, slice(t * 128, t * 128 + 128)), xslabs[nb].sl(k), k == 0, k == 3)
                  hs = Hb.sl(t, slice(nb * 512, nb * 512 + 512))
                  tt("dve", hs, psr(b), hs, ALU.add)
          ws_done(4)

          chk("G", ps_)
          norm_phase([Hb.sl(t) for t in range(NT)], [128 + 128 * t for t in range(NT)])
          SG = [Buf(sb, O_SCR + i * 4096, F32, [1024]) for i in range(2)]
          groups = [8, 8, 8, 8, 8, 4]
          j0 = 0
          it = 0
          for gi, G in enumerate(groups):
              AB = ACTb[gi % 2]
              for jl in range(G):
                  jj = j0 + jl
                  wcol = slice(0, 128)
                  i2 = it % 2
                  it += 1
                  slg = ws_get("w_f1", 0, 16, jj * 128, 128)
                  b = banks(2)
                  for k in range(16):
                      w = slg.sl(k, wcol)
                      mm(psr(b), w, UT.sl(k, slice(128, 640)), k == 0, k == 15)
                      mm(psr(b + 1), w, UT.sl(k, slice(640, 1152)), k == 0, k == 15)
                  ws_done()
                  act(SG[i2].sl(), psr2(b), AF.Silu)
                  slu = ws_get("w_f1", 0, 16, FFN + jj * 128, 128)
                  b = banks(2)
                  for k in range(16):
                      w = slu.sl(k, wcol)
                      mm(psr(b), w, UT.sl(k, slice(128, 640)), k == 0, k == 15)
                      mm(psr(b + 1), w, UT.sl(k, slice(640, 1152)), k == 0, k == 15)
                  ws_done()
                  tt("dve", AB.sl(jl), psr2(b), SG[i2].sl(), ALU.mult)
              if gi < len(groups) - 1:
                  for nb in range(4):
                      slab = ws_get("w_f2", j0, G, nb * 512, 512)
                      for t in range(NT):
                          b = banks(1)
                          for k in range(G):
                              mm(psr(b), AB.sl(k, slice(t * 128, t * 128 + 128)), slab.sl(k), k == 0, k == G - 1)
                          hs = Hb.sl(t, slice(nb * 512, nb * 512 + 512))
                          tt("dve", hs, psr(b), hs, ALU.add)
                      ws_done()
              else:
                  fslabs = [ws_get("w_f2", j0, G, nb * 512, 512) for nb in range(4)]
                  for t in range(NT):
                      for nb in range(4):
                          b = banks(1)
                          for k in range(G):
                              mm(psr(b), AB.sl(k, slice(t * 128, t * 128 + 128)), fslabs[nb].sl(k), k == 0, k == G - 1)
                          hs = Hb.sl(t, slice(nb * 512, nb * 512 + 512))
                          tt("dve", hs, psr(b), hs, ALU.add)
                  ws_done(4)
              j0 += G

          chk("H", ps_)
          load_g(4)
          for t in range(NT):
              rsc = norm_rstd1(Hb.sl(t))
              stt("dve", Hb.sl(t), Hb.sl(t), rsc, GB.sl(), ALU.mult, ALU.mult)
              key = ("o", t)
              if key not in out_keys:
                  out_keys.append(key)
              r0 = ps_ * T + t * 128
              dma("sp", yout[r0:r0 + 128, :], Hb.full[:, t, :], reads=Hb.sl(t).toks, key=key)
    except _Stop:
        pass

    if seq_in is not None:
        P.emit(nc, final_wait_keys=out_keys)
    st.close()
    nc._memmap = memmap
    return nc, seq_rec


_CACHE = {}


def _get_nc():
    if "nc" not in _CACHE:
        _, seq = _build(None)
        nc, _ = _build(seq)
        _CACHE["nc"] = nc
    return _CACHE["nc"]


def _host_inputs(x, mem, g_mix, w_in, conv_w, attn_sinks, w_attn_proj, w_conv_proj, w_mix_out,
                 g_xattn, g_mem, w_xq, w_xkv, w_xo, g_ffn, w_ffn_in, w_ffn_out, g_final):
    f32 = np.float32
    x = np.asarray(x, f32)
    mem = np.asarray(mem, f32)
    perm = []
    for c in range(8):
        for r in range(2):
            hq = (2 * (c // 4) + r) * 4 + (c % 4)
            perm.extend(range(hq * 64, hq * 64 + 64))
    perm = np.asarray(perm)
    w_in0 = np.asarray(w_in, f32)[0]
    w_in_p = np.ascontiguousarray(np.concatenate([w_in0[:, perm], w_in0[:, 1024:]], axis=1))
    w_ap_p = np.ascontiguousarray(np.asarray(w_attn_proj, f32)[0][perm, :])
    gvec = np.ascontiguousarray(np.stack([np.asarray(g_mix, f32)[0], np.asarray(g_xattn, f32)[0],
                                          np.asarray(g_mem, f32)[0], np.asarray(g_ffn, f32)[0],
                                          np.asarray(g_final, f32)]))
    cw = np.asarray(conv_w, f32)[0]
    cwin = np.ascontiguousarray(cw.reshape(3, 8, 128).transpose(2, 1, 0).reshape(128, 24))
    sinkin = np.ascontiguousarray(np.asarray(attn_sinks, f32)[0].reshape(1, 16))
    ident = np.eye(128, dtype=f32).astype(ml_dtypes.bfloat16)
    jj = np.arange(128)[:, None]
    ii = np.arange(128)[None, :]
    gen = np.stack([(jj > ii), (jj <= ii)]).astype(f32)
    genb = ((gen - 1.0) * 30000.0).transpose(2, 0, 1)
    half = 32
    inv_freq = (f32(10000.0) ** (-np.arange(half, dtype=f32) / f32(half))).astype(f32)
    shared = {"gvec": gvec, "w_in": w_in_p, "w_ap": w_ap_p,
              "w_cp": np.ascontiguousarray(np.asarray(w_conv_proj, f32)[0]),
              "w_mix": np.ascontiguousarray(np.asarray(w_mix_out, f32)[0]),
              "w_xq": np.ascontiguousarray(np.asarray(w_xq, f32)[0]),
              "w_xkv": np.ascontiguousarray(np.asarray(w_xkv, f32)[0]),
              "w_xo": np.ascontiguousarray(np.asarray(w_xo, f32)[0]),
              "w_f1": np.ascontiguousarray(np.asarray(w_ffn_in, f32)[0]),
              "w_f2": np.ascontiguousarray(np.asarray(w_ffn_out, f32)[0]),
              "cwin": cwin, "sinkin": sinkin, "identin": ident}
    maps = []
    for c in range(NCORES):
        b = c // 2
        xin = np.zeros((NPASS, TH, D), f32)
        rope = np.zeros((NPASS, 128, 2, 9, 32), f32)
        masks = np.zeros((128, 3, 2, 128), f32)
        masks[:, 0] = genb
        for p in range(NPASS):
            start = (c % 2) * TOK_CORE + p * T
            if start > 0:
                xin[p, 0:128] = x[b, start - 128:start]
            xin[p, 128:] = x[b, start:start + T]
            pos = (start - 128 + np.arange(9 * 128)).astype(f32)
            ang = (pos[:, None] * inv_freq[None, :]).astype(f32)
            cs = np.cos(ang).astype(f32).reshape(9, 128, 32)
            sn = np.sin(ang).astype(f32).reshape(9, 128, 32)
            rope[p, :, 0] = cs.transpose(1, 0, 2)
            rope[p, :, 1] = sn.transpose(1, 0, 2)
            m = genb.copy()
            if start == 0:
                m[:, 0, :] = -30000.0
            masks[:, 1 + p] = m
        d = dict(shared)
        d["xin"] = xin
        d["memin"] = np.ascontiguousarray(mem[b])
        d["ropein"] = rope
        d["maskin"] = masks.astype(ml_dtypes.bfloat16)
        maps.append(d)
    return maps


def kernel(**inputs):
    maps = _host_inputs(**inputs)
    nc = _get_nc()
    res = run_bass_kernel_spmd(nc, maps, core_ids=list(range(NCORES)))
    out = np.empty((BATCH, SEQ, D), np.float32)
    for c in range(NCORES):
        b = c // 2
        s0 = (c % 2) * TOK_CORE
        out[b, s0:s0 + TOK_CORE] = res.results[c]["yout"]
    return out
```

```python
import numpy as np
import ml_dtypes
import concourse.bass as bass
import concourse.mybir as mybir
from concourse.bass_utils import run_bass_kernel_spmd

F32 = mybir.dt.float32
BF16 = mybir.dt.bfloat16
U8 = mybir.dt.uint8
AF = mybir.ActivationFunctionType
ALU = mybir.AluOpType

UNIT = 512


class _Op:
    __slots__ = ("eng", "fn", "deps_c", "deps_d", "dma", "sig", "seq", "idx")


class Prog:
    ENGS = ("pe", "act", "dve", "pool", "sp")

    def __init__(self):
        self.ops = []
        self.last_w = {}
        self.readers = {}
        self.dma_count = {}

    def op(self, eng, fn, reads=(), writes=(), dma=None):
        o = _Op()
        o.eng, o.fn, o.idx = eng, fn, len(self.ops)
        o.dma = None
        o.sig = False
        o.seq = 0
        deps = set()
        lw, rd = self.last_w, self.readers
        for t in reads:
            w = lw.get(t)
            if w is not None:
                deps.add(w)
        for t in writes:
            w = lw.get(t)
            if w is not None:
                deps.add(w)
            r = rd.get(t)
            if r:
                deps.update(r.values())
        for t in writes:
            lw[t] = o.idx
            rd[t] = {}
        for t in reads:
            r = rd.get(t)
            if r is None:
                r = rd[t] = {}
            r[eng if dma is None else ("dma", o.idx)] = o.idx
        deps.discard(o.idx)
        dc, dd = {}, {}
        for d in deps:
            p = self.ops[d]
            if p.dma is not None:
                k, c = p.dma
                if dd.get(k, 0) < c:
                    dd[k] = c
            else:
                if p.eng == "pe" and eng == "pe" and dma is None:
                    continue
                if dc.get(p.eng, -1) < d:
                    dc[p.eng] = d
        o.deps_c, o.deps_d = dc, dd
        if dma is not None:
            c = self.dma_count.get(dma, 0) + 1
            self.dma_count[dma] = c
            o.dma = (dma, 16 * c)
        self.ops.append(o)
        return o

    def emit(self, nc, final_wait_keys=()):
        ops = self.ops
        for o in ops:
            for d in o.deps_c.values():
                ops[d].sig = True
        cnt = {e: 0 for e in self.ENGS}
        for o in ops:
            if o.dma is None and o.sig:
                cnt[o.eng] += 1
                o.seq = cnt[o.eng]
        from contextlib import ExitStack
        with ExitStack() as st:
            esem = {e: st.enter_context(nc.semaphore("s_" + e)) for e in self.ENGS}
            dsem = {k: st.enter_context(nc.semaphore("d_%d" % i)) for i, k in enumerate(self.dma_count)}
            block = st.enter_context(nc.Block())
            per = {e: [o for o in ops if o.eng == e] for e in self.ENGS}

            def body(ename):
                def run(e):
                    wc = {x: 0 for x in self.ENGS}
                    wd = {}
                    for o in per[ename]:
                        for pe_, d in o.deps_c.items():
                            need = ops[d].seq
                            if wc[pe_] < need:
                                e.wait_ge(esem[pe_], need)
                                wc[pe_] = need
                        for k, c in o.deps_d.items():
                            if wd.get(k, 0) < c:
                                e.wait_ge(dsem[k], c)
                                wd[k] = c
                        ins = o.fn(e)
                        if o.dma is not None:
                            ins.then_inc(dsem[o.dma[0]], 16)
                        elif o.sig:
                            ins.then_inc(esem[ename], 1)
                    if ename == "sp":
                        for k in final_wait_keys:
                            e.wait_ge(dsem[k], 16 * self.dma_count[k])
                return run

            block.tensor(body("pe"))
            block.scalar(body("act"))
            block.vector(body("dve"))
            block.gpsimd(body("pool"))
            block.sync(body("sp"))


D = 2048
SEQ = 4096
BATCH = 4
NCORES = 8
TOK_CORE = 2048
NPASS = 2
T = 1024
NT = 8
TH = T + 128
FFN = 5632
EPS = 1e-6
NSLOT = 4
SLOT_BYTES = 8192


class Ref:
    __slots__ = ("ap", "toks")

    def __init__(self, ap, toks):
        self.ap, self.toks = ap, toks


class Buf:
    def __init__(self, sb, off, dt, shape):
        self.sb, self.off, self.dt, self.shape = sb, off, dt, tuple(shape)
        self.esz = 2 if dt == BF16 else 4
        n = 1
        for s in shape:
            n *= s
        self.n = n
        self.nbytes = n * self.esz
        full = sb[:, off:off + self.nbytes].bitcast(dt)
        if len(shape) == 2:
            full = full.rearrange("p (a b) -> p a b", a=shape[0])
        elif len(shape) == 3:
            full = full.rearrange("p (a b c) -> p a b c", a=shape[0], b=shape[1])
        self.full = full

    def sl(self, *idx, p=(0, 128)):
        shape = self.shape
        idx = list(idx) + [slice(None)] * (len(shape) - len(idx))
        rng = []
        for i, s in zip(idx, shape):
            if isinstance(i, int):
                rng.append((i, i + 1))
            else:
                a = 0 if i.start is None else i.start
                b = s if i.stop is None else i.stop
                rng.append((a, b))
        ap = self.full[(slice(p[0], p[1]),) + tuple(idx)]
        strides = [1] * len(shape)
        for d in range(len(shape) - 2, -1, -1):
            strides[d] = strides[d + 1] * shape[d + 1]
        toks = set()
        lead = rng[:-1]
        la, lb = rng[-1]

        def rec(d, base):
            if d == len(lead):
                b0 = self.off + (base + la) * self.esz
                b1 = self.off + (base + lb) * self.esz - 1
                for u in range(b0 // UNIT, b1 // UNIT + 1):
                    toks.add(u)
                return
            for i in range(lead[d][0], lead[d][1]):
                rec(d + 1, base + i * strides[d])

        rec(0, 0)
        return Ref(ap, list(toks))


class _Stop(Exception):
    pass


def _build(seq_in=None, stop=None):
    nc = bass.Bass("TRN2", target_bir_lowering=False)
    P = Prog()
    dram = {}

    def din(name, shape, dt=F32):
        dram[name] = nc.dram_tensor(name, list(shape), dt, kind="ExternalInput").ap()
        return dram[name]

    xin = din("xin", [NPASS, TH, D])
    memin = din("memin", [256, D])
    gvec = din("gvec", [5, D])
    w_in = din("w_in", [D, 8704])
    w_ap = din("w_ap", [1024, D])
    w_cp = din("w_cp", [1024, D])
    w_mix = din("w_mix", [D, D])
    w_xq = din("w_xq", [D, 512])
    w_xkv = din("w_xkv", [D, 1024])
    w_xo = din("w_xo", [512, D])
    w_f1 = din("w_f1", [D, 2 * FFN])
    w_f2 = din("w_f2", [FFN, D])
    cwin = din("cwin", [128, 24])
    sinkin = din("sinkin", [1, 16])
    ropein = din("ropein", [NPASS, 128, 2, 9, 32])
    maskin = din("maskin", [128, 3, 2, 128], BF16)
    identin = din("identin", [128, 128], BF16)
    yout = nc.dram_tensor("yout", [TOK_CORE, D], F32, kind="ExternalOutput").ap()
    dbg = nc.dram_tensor("dbg", [128, 206 * 1024], U8, kind="ExternalOutput").ap() if stop else None
    memmap = {}
    wd = {"w_in": w_in, "w_ap": w_ap, "w_cp": w_cp, "w_mix": w_mix, "w_xq": w_xq, "w_xkv": w_xkv,
          "w_xo": w_xo, "w_f1": w_f1, "w_f2": w_f2}
    wview = {k: v.rearrange("(k p) c -> p k c", p=128) for k, v in wd.items()}

    from contextlib import ExitStack
    st = ExitStack()
    SBYTES = 206 * 1024
    sb = st.enter_context(nc.sbuf_tensor("sb", [128, SBYTES], U8))
    ps = st.enter_context(nc.psum_tensor("ps", [128, 8, 512], F32))

    off = [0]

    def alloc(nbytes, align=64):
        o = (off[0] + align - 1) // align * align
        off[0] = o + nbytes
        assert off[0] <= SBYTES, ("SBUF overflow", off[0])
        return o

    O_H = alloc(NT * 8192, 512)
    O_UT = alloc(16 * TH * 2, 512)
    O_R3 = alloc(32768, 512)
    O_WS = alloc(NSLOT * SLOT_BYTES, 512)
    O_SCR = alloc(16384, 512)
    O_GB = alloc(8192, 512)
    O_UB = alloc(4096, 512)
    O_ROPE = alloc(2 * 9 * 32 * 4, 512)
    O_MASK = alloc(3 * 2 * 128 * 2, 512)
    O_ID = alloc(256, 512)
    O_ONES = alloc(256, 512)
    O_MEMK = alloc(4 * 256 * 2, 512)
    O_MEMV = alloc(2 * 512 * 2, 512)
    O_SS = alloc(80 * 4, 512)
    O_RS = alloc(80 * 4, 512)
    O_CW = alloc(24 * 4, 512)
    O_SINK = alloc(16 * 4, 64)
    O_SK2 = alloc(2 * 4 * 4, 64)
    O_I4 = alloc(1024, 512)
    O_ONESP = alloc(512, 512)

    Hb = Buf(sb, O_H, F32, [NT, D])
    UT = Buf(sb, O_UT, BF16, [16, TH])
    GB = Buf(sb, O_GB, F32, [D])
    UB = Buf(sb, O_UB, BF16, [D])
    ROPE = Buf(sb, O_ROPE, F32, [2, 9, 32])
    MASK = Buf(sb, O_MASK, BF16, [3, 2, 128])
    IDT = Buf(sb, O_ID, BF16, [128])
    ONES = Buf(sb, O_ONES, BF16, [128])
    MEMK = Buf(sb, O_MEMK, BF16, [4, 256])
    MEMV = Buf(sb, O_MEMV, BF16, [2, 512])
    SS = Buf(sb, O_SS, F32, [80])
    RS = Buf(sb, O_RS, F32, [80])
    CW = Buf(sb, O_CW, F32, [24])
    SINK = Buf(sb, O_SINK, F32, [16])
    SK2 = Buf(sb, O_SK2, F32, [2, 4])
    I4 = Buf(sb, O_I4, BF16, [512])
    ONESP = Buf(sb, O_ONESP, BF16, [2, 128])
    AO = Buf(sb, O_H + 4 * 8192, BF16, [8, T])
    CV = Buf(sb, O_H + 6 * 8192, BF16, [8, T])
    QT = Buf(sb, O_R3, BF16, [8, T])
    KT = Buf(sb, O_R3 + 16384, BF16, [2, TH])
    VA = Buf(sb, O_R3 + 16384 + 4608, BF16, [9, 4, 128])
    MG = Buf(sb, O_R3, BF16, [16, T])
    XQ = Buf(sb, O_R3, BF16, [4, T])
    XO = Buf(sb, O_R3 + 8192, BF16, [4, T])
    ACTb = [Buf(sb, O_R3 + i * 16384, BF16, [8, T]) for i in range(2)]
    XH = Buf(sb, O_SCR, F32, [D])
    JUNK = Buf(sb, O_SCR + 8192, BF16, [D])

    def psr(b, c0=0, c1=512, p=(0, 128)):
        return Ref(ps[p[0]:p[1], b, c0:c1], [("ps", b)])

    def psr2(b, p=(0, 128)):
        return Ref(ps[p[0]:p[1], b:b + 2, :].rearrange("p a b -> p (a b)"), [("ps", b), ("ps", b + 1)])

    def psbf(b, c0, c1):
        return Ref(ps[:, b, :].bitcast(BF16)[:, c0:c1], [("ps", b)])

    bank_ctr = [0]

    def banks(n=1):
        b = bank_ctr[0]
        if n == 2 and b % 2:
            b += 1
        if b + n > 8:
            b = 0
        bank_ctr[0] = (b + n) % 8
        return b

    def mm(out, lhsT, rhs, start, stop):
        P.op("pe", lambda e: e.matmul(out.ap, lhsT=lhsT.ap, rhs=rhs.ap, start=start, stop=stop),
             reads=lhsT.toks + rhs.toks, writes=out.toks)

    def tr(out, in_):
        P.op("pe", lambda e: e.transpose(out=out.ap, in_=in_.ap, identity=IDT.full),
             reads=in_.toks + IDT.sl().toks, writes=out.toks)

    def act(out, in_, func, scale=1.0, bias=0.0, accum=None):
        sc = scale.ap if isinstance(scale, Ref) else scale
        rd = in_.toks + (scale.toks if isinstance(scale, Ref) else [])
        wr = out.toks + (accum.toks if accum is not None else [])
        if accum is None:
            P.op("act", lambda e: e.activation(out=out.ap, in_=in_.ap, func=func, bias=bias, scale=sc),
                 reads=rd, writes=wr)
        else:
            P.op("act", lambda e: e.activation(out=out.ap, in_=in_.ap, func=func, bias=bias, scale=sc,
                                               accum_out=accum.ap), reads=rd, writes=wr)

    def cp(eng, out, in_):
        if eng == "act":
            P.op("act", lambda e: e.copy(out=out.ap, in_=in_.ap), reads=in_.toks, writes=out.toks)
        else:
            P.op(eng, lambda e: e.tensor_copy(out=out.ap, in_=in_.ap), reads=in_.toks, writes=out.toks)

    def tt(eng, out, in0, in1, op):
        P.op(eng, lambda e: e.tensor_tensor(out=out.ap, in0=in0.ap, in1=in1.ap, op=op),
             reads=in0.toks + in1.toks, writes=out.toks)

    def ts(eng, out, in0, s1, s2, op0, op1=None):
        a1 = s1.ap if isinstance(s1, Ref) else s1
        a2 = s2.ap if isinstance(s2, Ref) else s2
        rd = in0.toks + (s1.toks if isinstance(s1, Ref) else []) + (s2.toks if isinstance(s2, Ref) else [])
        if op1 is None:
            P.op(eng, lambda e: e.tensor_scalar(out=out.ap, in0=in0.ap, scalar1=a1, scalar2=None, op0=op0),
                 reads=rd, writes=out.toks)
        else:
            P.op(eng, lambda e: e.tensor_scalar(out=out.ap, in0=in0.ap, scalar1=a1, scalar2=a2, op0=op0, op1=op1),
                 reads=rd, writes=out.toks)

    def stt(eng, out, in0, scalar, in1, op0, op1):
        a = scalar.ap if isinstance(scalar, Ref) else scalar
        rd = in0.toks + in1.toks + (scalar.toks if isinstance(scalar, Ref) else [])
        P.op(eng, lambda e: e.scalar_tensor_tensor(out=out.ap, in0=in0.ap, scalar=a, in1=in1.ap, op0=op0, op1=op1),
             reads=rd, writes=out.toks)

    def memset(eng, out, val):
        P.op(eng, lambda e: e.memset(out.ap, val), writes=out.toks)

    dma_ctr = [0]

    def dma(eng, out_ap, in_ap, reads=(), writes=(), key=None):
        if key is None:
            dma_ctr[0] += 1
            key = ("m", dma_ctr[0] % 6)
        P.op(eng, lambda e: e.dma_start(out=out_ap, in_=in_ap), reads=list(reads), writes=list(writes), dma=key)

    RING = NSLOT * SLOT_BYTES
    seq_rec = []
    ws = {"get": 0, "emit": 0, "rel": 0}
    ws_offs, ws_need = [], []
    if seq_in is not None:
        head = 0
        for (_n, _k0, nk, _c0, ncols) in seq_in:
            size = nk * ncols * 2
            o = (head + size - 1) // size * size
            if o + size > RING:
                o = 0
            ws_offs.append(o)
            head = o + size
        for j, dj in enumerate(seq_in):
            sj = dj[2] * dj[4] * 2
            nd = -1
            for i in range(j - 1, max(-1, j - 40), -1):
                si = seq_in[i][2] * seq_in[i][4] * 2
                if ws_offs[i] < ws_offs[j] + sj and ws_offs[j] < ws_offs[i] + si:
                    nd = i
                    break
            ws_need.append(nd)

    def ws_dma(name, k0, nk, c0, ncols, o):
        b = Buf(sb, O_WS + o, BF16, [nk, ncols])
        r = b.sl()
        src = wview[name][:, k0:k0 + nk, c0:c0 + ncols]
        P.op("pool", lambda e: e.dma_start(out=r.ap, in_=src), writes=r.toks, dma=("ws", o // 2048))
        return b

    def ws_pump():
        while ws["emit"] < len(seq_in) and ws_need[ws["emit"]] < ws["rel"]:
            j = ws["emit"]
            ws_dma(*seq_in[j], ws_offs[j])
            ws["emit"] = j + 1

    def ws_get(name, k0, nk, c0, ncols):
        assert nk * ncols * 2 <= SLOT_BYTES
        d = (name, k0, nk, c0, ncols)
        seq_rec.append(d)
        j = ws["get"]
        ws["get"] = j + 1
        if seq_in is None:
            return ws_dma(*d, 0)
        assert seq_in[j] == d, (j, seq_in[j], d)
        ws_pump()
        assert ws["emit"] > j, ("weight ring too small for slabs held at once", j)
        return Buf(sb, O_WS + ws_offs[j], BF16, [nk, ncols])

    def ws_done(n=1):
        ws["rel"] += n
        if seq_in is not None:
            ws_pump()

    dma("sp", IDT.full, identin, writes=IDT.sl().toks, key="c0")
    dma("sp", MASK.full, maskin, writes=MASK.sl().toks, key="c1")
    dma("sp", CW.full, cwin, writes=CW.sl().toks, key="c2")
    dma("sp", SINK.full, sinkin[0:1, :].broadcast_to([128, 16]), writes=SINK.sl().toks, key="c3")
    memset("dve", ONES.sl(), 1.0)
    memset("dve", SS.sl(), 0.0)
    act(SINK.sl(), SINK.sl(), AF.Exp)
    for kc_ in range(2):
        for hh_ in range(2):
            pr_ = (64 * hh_, 64 * hh_ + 64)
            hk_ = 2 * kc_ + hh_
            cp("dve", SK2.sl(kc_, p=pr_), SINK.sl(slice(hk_ * 4, hk_ * 4 + 4), p=pr_))
    for g_ in range(4):
        cp("dve", I4.sl(slice(g_ * 128, g_ * 128 + 128)), IDT.sl())
    memset("dve", ONESP.sl(), 0.0)
    memset("dve", ONESP.sl(0, slice(0, 64)), 1.0)
    memset("dve", ONESP.sl(1, slice(64, 128)), 1.0)
    stat_ctr = [0]

    def load_g(i):
        dma("sp", GB.full, gvec[i:i + 1, :].broadcast_to([128, D]), writes=GB.sl().toks, key="g")

    UB2 = Buf(sb, O_SCR + 12288, BF16, [D])

    def norm_rstd(xts):
        n = len(xts)
        c0 = stat_ctr[0]
        stat_ctr[0] += n
        for i, xt in enumerate(xts):
            act(JUNK.sl(), xt, AF.Square, accum=SS.sl(slice(c0 + i, c0 + i + 1)))
        blk = RS.sl(slice(c0, c0 + n))
        ts("dve", blk, SS.sl(slice(c0, c0 + n)), 1.0 / D, EPS, ALU.mult, ALU.add)
        act(blk, blk, AF.Sqrt)
        P.op("dve", lambda e: e.reciprocal(out=blk.ap, in_=blk.ap), reads=blk.toks, writes=blk.toks)
        return [RS.sl(slice(c0 + i, c0 + i + 1)) for i in range(n)]

    def norm_rstd1(xt):
        c = stat_ctr[0]
        stat_ctr[0] += 1
        ssc = SS.sl(slice(c, c + 1))
        rsc = RS.sl(slice(c, c + 1))
        act(JUNK.sl(), xt, AF.Square, accum=ssc)
        ts("dve", rsc, ssc, 1.0 / D, EPS, ALU.mult, ALU.add)
        act(rsc, rsc, AF.Sqrt)
        P.op("dve", lambda e: e.reciprocal(out=rsc.ap, in_=rsc.ap), reads=rsc.toks, writes=rsc.toks)
        return rsc

    def norm_phase(xts, cols):
        rs = norm_rstd(xts)
        for i, (xt, col0) in enumerate(zip(xts, cols)):
            ub = UB if i % 2 == 0 else UB2
            stt("dve", ub.sl(), xt, rs[i], GB.sl(), ALU.mult, ALU.mult)
            for half in range(2):
                b = banks(1)
                for j in range(8):
                    k = half * 8 + j
                    tr(psbf(b, j * 128, (j + 1) * 128), ub.sl(slice(k * 128, (k + 1) * 128)))
                src = Ref(ps[:, b, :].bitcast(BF16).rearrange("p (a b) -> p a b", a=8), [("ps", b)])
                cp("act", UT.sl(slice(half * 8, half * 8 + 8), slice(col0, col0 + 128)), src)

    load_g(2)
    for mt in range(2):
        dma("sp", XH.full, memin[mt * 128:(mt + 1) * 128, :], writes=XH.sl().toks, key="xh")
        norm_phase([XH.sl()], [mt * 128])
    for s in range(2):
        slab = ws_get("w_xkv", 0, 16, s * 256, 256)
        for hh in range(2):
            hx = s * 2 + hh
            b = banks(1)
            for k in range(16):
                mm(psr(b, 0, 256), slab.sl(k, slice(hh * 128, hh * 128 + 128)), UT.sl(k, slice(0, 256)), k == 0, k == 15)
            cp("act", MEMK.sl(hx), psr(b, 0, 256))
        ws_done()
    for s in range(2):
        slab = ws_get("w_xkv", 0, 16, 512 + s * 256, 256)
        for mt in range(2):
            b = banks(1)
            for k in range(16):
                mm(psr(b, 0, 256), UT.sl(k, slice(mt * 128, mt * 128 + 128)), slab.sl(k), k == 0, k == 15)
            cp("dve", MEMV.sl(mt, slice(s * 256, s * 256 + 256)), psr(b, 0, 256))
        ws_done()

    out_keys = []

    def chk(name, ps_):
        if stop == (name, ps_):
            allt = list(range(0, SBYTES // UNIT))
            dma("sp", dbg, sb[:, :], reads=allt, key="dbg")
            out_keys.append("dbg")
            raise _Stop()

    memmap.update(dict(O_H=O_H, O_UT=O_UT, O_R3=O_R3, O_SCR=O_SCR, O_MEMK=O_MEMK, O_MEMV=O_MEMV, O_RS=O_RS, O_SS=O_SS,
                       O_SINK=O_SINK, O_GB=O_GB, O_UB=O_UB, O_ROPE=O_ROPE, O_MASK=O_MASK))
    try:
      for ps_ in range(NPASS):
          dma("sp", ROPE.full, ropein[ps_], writes=ROPE.sl().toks, key="rope")
          load_g(0)
          dma("sp", XH.full, xin[ps_, 0:128, :], writes=XH.sl().toks, key="xh")
          for t in range(NT):
              dma("sp", Hb.full[:, t, :], xin[ps_, 128 + t * 128:256 + t * 128, :], writes=Hb.sl(t).toks, key=("x", t))
          norm_phase([XH.sl()] + [Hb.sl(t) for t in range(NT)], [128 * t for t in range(NT + 1)])

          chk("A", ps_)
          memset("dve", VA.sl(), 0.0)
          SC_A = Buf(sb, O_SCR, F32, [2, 256])
          SC_B = Buf(sb, O_SCR + 2048, F32, [2, 256])
          SC_Q = Buf(sb, O_SCR + 4096, BF16, [2, 256])
          it = 0
          pend = []
          for s in range(6):
              slab = ws_get("w_in", 0, 16, s * 256, 256)
              for t in range(9):
                  if s < 4 and t == 0:
                      continue
                  b = banks(1)
                  for k in range(16):
                      mm(psr(b, 0, 256), UT.sl(k, slice(t * 128, t * 128 + 128)), slab.sl(k), k == 0, k == 15)
                  if s == 5:
                      pv = ps[:, b, 0:256].rearrange("p (a b c) -> p a b c", a=2, b=2)
                      va = VA.full[:, t, :, :].rearrange("p (a b) c -> p a b c", a=2)
                      P.op("act", lambda e, pv=pv, va=va: e.copy(out=va[:, :, 0, 0:64], in_=pv[:, :, 0, :]),
                           reads=[("ps", b)], writes=VA.sl(t).toks)
                      P.op("dve", lambda e, pv=pv, va=va: e.tensor_copy(out=va[:, :, 1, 64:128], in_=pv[:, :, 1, :]),
                           reads=[("ps", b)], writes=VA.sl(t).toks)
                      continue
                  i2 = it % 2
                  it += 1
                  A_ = SC_A.sl(i2)
                  B_ = SC_B.sl(i2)
                  Q_ = SC_Q.sl(i2)
                  pq = ps[:, b, 0:256].rearrange("p (h two f) -> p h two f", h=4, two=2)
                  cosb = ROPE.full[:, 0, t, :].unsqueeze(1).unsqueeze(1).broadcast_to([128, 4, 2, 32])
                  sinb = ROPE.full[:, 1, t, :].unsqueeze(1).broadcast_to([128, 4, 32])
                  Aap = SC_A.full[:, i2, :].rearrange("p (h two f) -> p h two f", h=4, two=2)
                  Bap = SC_B.full[:, i2, :].rearrange("p (h two f) -> p h two f", h=4, two=2)
                  rt = ROPE.sl().toks
                  P.op("dve", lambda e, Aap=Aap, pq=pq, cosb=cosb: e.tensor_tensor(out=Aap, in0=pq, in1=cosb, op=ALU.mult),
                       reads=[("ps", b)] + rt, writes=A_.toks)
                  P.op("dve", lambda e, Bap=Bap, pq=pq, sinb=sinb: e.scalar_tensor_tensor(
                      out=Bap[:, :, 0, :], in0=pq[:, :, 1, :], scalar=-1.0, in1=sinb, op0=ALU.mult, op1=ALU.mult),
                      reads=[("ps", b)] + rt, writes=B_.toks)
                  P.op("dve", lambda e, Bap=Bap, pq=pq, sinb=sinb: e.tensor_tensor(
                      out=Bap[:, :, 1, :], in0=pq[:, :, 0, :], in1=sinb, op=ALU.mult),
                      reads=[("ps", b)] + rt, writes=B_.toks)
                  tt("dve", Q_, A_, B_, ALU.add)

                  def fin(s=s, t=t, i2=i2):
                      b2 = banks(1)
                      for j in range(2):
                          tr(psbf(b2, j * 128, (j + 1) * 128), SC_Q.sl(i2, slice(j * 128, (j + 1) * 128)))
                      src = Ref(ps[:, b2, :].bitcast(BF16)[:, 0:256].rearrange("p (a b) -> p a b", a=2), [("ps", b2)])
                      if s < 4:
                          cp("act", QT.sl(slice(2 * s, 2 * s + 2), slice((t - 1) * 128, t * 128)), src)
                      else:
                          cp("act", KT.sl(slice(0, 2), slice(t * 128, (t + 1) * 128)), src)
                  pend.append(fin)
                  if len(pend) > 1:
                      pend.pop(0)()
              ws_done()
          while pend:
              pend.pop(0)()

          chk("B", ps_)
          PT = [Buf(sb, O_SCR + i * 4096, BF16, [2, 2, 512]) for i in range(2)]
          RC = [Buf(sb, O_SCR + 8192 + i * 2048, F32, [512]) for i in range(2)]
          iters = [(n, kc) for n in range(NT) for kc in range(2)]

          def qk(n, kc, i2):
            mi = (1 + ps_) if n == 0 else 0
            for hh in range(2):
                pr = (64 * hh, 64 * hh + 64)
                for j in range(2):
                    bk = 2 * hh + j
                    mm(psr(bk), KT.sl(kc, slice((n + j) * 128, (n + j + 1) * 128), p=pr),
                       QT.sl(slice(4 * kc, 4 * kc + 4), slice(n * 128, (n + 1) * 128), p=pr), True, False)
                    mm(psr(bk), MASK.sl(mi, j), I4.sl(), False, True)
                act(PT[i2].sl(hh), Ref(ps[:, 2 * hh:2 * hh + 2, :], [("ps", 2 * hh), ("ps", 2 * hh + 1)]), AF.Exp, scale=0.125)

          def pv(n, kc, i2):
            bo = 4 + 2 * i2
            for q_ in range(4):
                hh, j = q_ // 2, q_ % 2
                mm(psr(bo), VA.sl(n + j, 2 * kc + hh), PT[i2].sl(hh, j), q_ == 0, q_ == 3)
            for q_ in range(4):
                hh, j = q_ // 2, q_ % 2
                mm(psr(bo + 1), ONESP.sl(hh), PT[i2].sl(hh, j), q_ == 0, q_ == 3)
            skb = SK2.full[:, kc, :].unsqueeze(2).broadcast_to([128, 4, 128])
            rc = RC[i2].full.rearrange("p (g q) -> p g q", g=4)
            pd = ps[:, bo + 1, :].rearrange("p (g q) -> p g q", g=4)
            P.op("dve", lambda e: e.tensor_tensor(out=rc, in0=pd, in1=skb, op=ALU.add),
                 reads=[("ps", bo + 1)] + SK2.sl().toks, writes=RC[i2].sl().toks)
            act(RC[i2].sl(), RC[i2].sl(), AF.Ln)
            act(RC[i2].sl(), RC[i2].sl(), AF.Exp, scale=-1.0)
            po = ps[:, bo, :].rearrange("p (g q) -> p g q", g=4)
            ao = AO.sl(slice(4 * kc, 4 * kc + 4), slice(n * 128, (n + 1) * 128))
            P.op("dve", lambda e: e.tensor_tensor(out=ao.ap, in0=po, in1=rc, op=ALU.mult),
                 reads=[("ps", bo)] + RC[i2].sl().toks, writes=ao.toks)

          for i, (n, kc) in enumerate(iters):
              qk(n, kc, i % 2)
              if i > 0:
                  pv(*iters[i - 1], (i - 1) % 2)
          pv(*iters[-1], (len(iters) - 1) % 2)
          bank_ctr[0] = 0

          chk("C", ps_)
          ZS = Buf(sb, O_SCR, F32, [1026])
          CZ = Buf(sb, O_SCR + 4608, F32, [1026])
          AC = Buf(sb, O_SCR + 9216, F32, [1024])
          for c in range(8):
              wcol = slice(0, 128)
              slz = ws_get("w_in", 0, 16, 1536 + c * 128, 128)
              bz = banks(2)
              bzh = banks(1)
              for k in range(16):
                  w = slz.sl(k, wcol)
                  mm(psr(bz), w, UT.sl(k, slice(128, 640)), k == 0, k == 15)
                  mm(psr(bz + 1), w, UT.sl(k, slice(640, 1152)), k == 0, k == 15)
                  mm(psr(bzh, 0, 2), w, UT.sl(k, slice(126, 128)), k == 0, k == 15)
              ws_done()
              cp("act", ZS.sl(slice(2, 1026)), psr2(bz))
              cp("act", ZS.sl(slice(0, 2)), psr(bzh, 0, 2))
              slg = ws_get("w_in", 0, 16, 3584 + c * 128, 128)
              bg = banks(2)
              bgh = banks(1)
              for k in range(16):
                  w = slg.sl(k, wcol)
                  mm(psr(bg), w, UT.sl(k, slice(128, 640)), k == 0, k == 15)
                  mm(psr(bg + 1), w, UT.sl(k, slice(640, 1152)), k == 0, k == 15)
                  mm(psr(bgh, 0, 2), w, UT.sl(k, slice(126, 128)), k == 0, k == 15)
              ws_done()
              tt("dve", CZ.sl(slice(2, 1026)), psr2(bg), ZS.sl(slice(2, 1026)), ALU.mult)
              tt("dve", CZ.sl(slice(0, 2)), psr(bgh, 0, 2), ZS.sl(slice(0, 2)), ALU.mult)
              slb = ws_get("w_in", 0, 16, 2560 + c * 128, 128)
              bb = banks(2)
              for k in range(16):
                  w = slb.sl(k, wcol)
                  mm(psr(bb), w, UT.sl(k, slice(128, 640)), k == 0, k == 15)
                  mm(psr(bb + 1), w, UT.sl(k, slice(640, 1152)), k == 0, k == 15)
              ws_done()
              act(AC.sl(), CZ.sl(slice(0, 1024)), AF.Copy, scale=CW.sl(slice(c * 3, c * 3 + 1)))
              stt("dve", AC.sl(), CZ.sl(slice(1, 1025)), CW.sl(slice(c * 3 + 1, c * 3 + 2)), AC.sl(), ALU.mult, ALU.add)
              stt("dve", AC.sl(), CZ.sl(slice(2, 1026)), CW.sl(slice(c * 3 + 2, c * 3 + 3)), AC.sl(), ALU.mult, ALU.add)
              tt("dve", CV.sl(c), psr2(bb), AC.sl(), ALU.mult)

          chk("D", ps_)
          SGA = Buf(sb, O_SCR, F32, [1024])
          M1 = Buf(sb, O_SCR + 4096, F32, [1024])
          SGC = Buf(sb, O_SCR + 8192, F32, [1024])
          for f in range(16):
              wcol = slice(0, 128)
              sla = ws_get("w_in", 0, 16, 4608 + f * 128, 128)
              b = banks(2)
              for k in range(16):
                  w = sla.sl(k, wcol)
                  mm(psr(b), w, UT.sl(k, slice(128, 640)), k == 0, k == 15)
                  mm(psr(b + 1), w, UT.sl(k, slice(640, 1152)), k == 0, k == 15)
              ws_done()
              act(SGA.sl(), psr2(b), AF.Sigmoid)
              slp = ws_get("w_ap", 0, 8, f * 128, 128)
              b = banks(2)
              for k in range(8):
                  w = slp.sl(k, wcol)
                  mm(psr(b), w, AO.sl(k, slice(0, 512)), k == 0, k == 7)
                  mm(psr(b + 1), w, AO.sl(k, slice(512, 1024)), k == 0, k == 7)
              ws_done()
              tt("dve", M1.sl(), psr2(b), SGA.sl(), ALU.mult)
              slc = ws_get("w_in", 0, 16, 6656 + f * 128, 128)
              b = banks(2)
              for k in range(16):
                  w = slc.sl(k, wcol)
                  mm(psr(b), w, UT.sl(k, slice(128, 640)), k == 0, k == 15)
                  mm(psr(b + 1), w, UT.sl(k, slice(640, 1152)), k == 0, k == 15)
              ws_done()
              act(SGC.sl(), psr2(b), AF.Sigmoid)
              slq = ws_get("w_cp", 0, 8, f * 128, 128)
              b = banks(2)
              for k in range(8):
                  w = slq.sl(k, wcol)
                  mm(psr(b), w, CV.sl(k, slice(0, 512)), k == 0, k == 7)
                  mm(psr(b + 1), w, CV.sl(k, slice(512, 1024)), k == 0, k == 7)
              ws_done()
              tt("dve", SGC.sl(), psr2(b), SGC.sl(), ALU.mult)
              tt("dve", MG.sl(f), M1.sl(), SGC.sl(), ALU.add)

          chk("E", ps_)
          for t in range(4, NT):
              dma("sp", Hb.full[:, t, :], xin[ps_, 128 + t * 128:256 + t * 128, :], writes=Hb.sl(t).toks, key=("x", t))
          load_g(1)
          for nb in range(4):
              for kh in range(2):
                  slab = ws_get("w_mix", kh * 8, 8, nb * 512, 512)
                  for t in range(NT):
                      b = banks(1)
                      for k in range(8):
                          mm(psr(b), MG.sl(kh * 8 + k, slice(t * 128, t * 128 + 128)), slab.sl(k), k == 0, k == 7)
                      hs = Hb.sl(t, slice(nb * 512, nb * 512 + 512))
                      tt("dve", hs, psr(b), hs, ALU.add)
                  ws_done()

          chk("F", ps_)
          norm_phase([Hb.sl(t) for t in range(NT)], [128 + 128 * t for t in range(NT)])
          for s in range(2):
              slab = ws_get("w_xq", 0, 16, s * 256, 256)
              for hh in range(2):
                  hx = 2 * s + hh
                  b = banks(2)
                  for k in range(16):
                      w = slab.sl(k, slice(hh * 128, hh * 128 + 128))
                      mm(psr(b), w, UT.sl(k, slice(128, 640)), k == 0, k == 15)
                      mm(psr(b + 1), w, UT.sl(k, slice(640, 1152)), k == 0, k == 15)
                  cp("act", XQ.sl(hx), psr2(b))
              ws_done()
          PX = [Buf(sb, O_SCR + i * 2048, BF16, [2, 512]) for i in range(2)]
          RX = [Buf(sb, O_SCR + 4096 + i * 2048, F32, [512]) for i in range(2)]
          it = 0
          for hx in range(4):
              for th in range(2):
                  i2 = it % 2
                  it += 1
                  b = banks(2)
                  for mt in range(2):
                      mm(psr(b + mt), MEMK.sl(hx, slice(mt * 128, mt * 128 + 128)), XQ.sl(hx, slice(th * 512, th * 512 + 512)), True, True)
                  act(PX[i2].sl(), Ref(ps[:, b:b + 2, :], [("ps", b), ("ps", b + 1)]), AF.Exp, scale=float(128 ** -0.5))
                  bo = banks(2)
                  for mt in range(2):
                      mm(psr(bo), MEMV.sl(mt, slice(hx * 128, hx * 128 + 128)), PX[i2].sl(mt), mt == 0, mt == 1)
                  for mt in range(2):
                      mm(psr(bo + 1), ONES.sl(), PX[i2].sl(mt), mt == 0, mt == 1)
                  act(RX[i2].sl(), psr(bo + 1), AF.Ln)
                  act(RX[i2].sl(), RX[i2].sl(), AF.Exp, scale=-1.0)
                  tt("dve", XO.sl(hx, slice(th * 512, th * 512 + 512)), psr(bo), RX[i2].sl(), ALU.mult)
          load_g(3)
          xslabs = [ws_get("w_xo", 0, 4, nb * 512, 512) for nb in range(4)]
          for t in range(NT):
              for nb in range(4):
                  b = banks(1)
                  for k in range(4):
                      mm(psr(b), XO.sl(k, slice(t * 128, t * 128 + 128)), xslabs[nb].sl(k), k == 0, k == 3)
                  hs = Hb.sl(t, slice(nb * 512, nb * 512 + 512))
                  tt("dve", hs, psr(b), hs, ALU.add)
          ws_done(4)

          chk("G", ps_)
          norm_phase([Hb.sl(t) for t in range(NT)], [128 + 128 * t for t in range(NT)])
          SG = [Buf(sb, O_SCR + i * 4096, F32, [1024]) for i in range(2)]
          groups = [8, 8, 8, 8, 8, 4]
          j0 = 0
          it = 0
          for gi, G in enumerate(groups):
              AB = ACTb[gi % 2]
              for jl in range(G):
                  jj = j0 + jl
                  wcol = slice(0, 128)
                  i2 = it % 2
                  it += 1
                  slg = ws_get("w_f1", 0, 16, jj * 128, 128)
                  b = banks(2)
                  for k in range(16):
                      w = slg.sl(k, wcol)
                      mm(psr(b), w, UT.sl(k, slice(128, 640)), k == 0, k == 15)
                      mm(psr(b + 1), w, UT.sl(k, slice(640, 1152)), k == 0, k == 15)
                  ws_done()
                  act(SG[i2].sl(), psr2(b), AF.Silu)
                  slu = ws_get("w_f1", 0, 16, FFN + jj * 128, 128)
                  b = banks(2)
                  for k in range(16):
                      w = slu.sl(k, wcol)
                      mm(psr(b), w, UT.sl(k, slice(128, 640)), k == 0, k == 15)
                      mm(psr(b + 1), w, UT.sl(k, slice(640, 1152)), k == 0, k == 15)
                  ws_done()
                  tt("dve", AB.sl(jl), psr2(b), SG[i2].sl(), ALU.mult)
              if gi < len(groups) - 1:
                  for nb in range(4):
                      slab = ws_get("w_f2", j0, G, nb * 512, 512)
                      for t in range(NT):
                          b = banks(1)
                          for k in range(G):
                              mm(psr(b), AB.sl(k, slice(t * 128, t * 128 + 128)), slab.sl(k), k == 0, k == G - 1)
                          hs = Hb.sl(t, slice(nb * 512, nb * 512 + 512))
                          tt("dve", hs, psr(b), hs, ALU.add)
                      ws_done()
              else:
                  fslabs = [ws_get("w_f2", j0, G, nb * 512, 512) for nb in range(4)]
                  for t in range(NT):
                      for nb in range(4):
                          b = banks(1)
                          for k in range(G):
                              mm(psr(b), AB.sl(k, slice(t * 128, t * 128 + 128)), fslabs[nb].sl(k), k == 0, k == G - 1)
                          hs = Hb.sl(t, slice(nb * 512, nb * 512 + 512))
                          tt("dve", hs, psr(b), hs, ALU.add)
                  ws_done(4)
              j0 += G

          chk("H", ps_)
          load_g(4)
          for t in range(NT):
              rsc = norm_rstd1(Hb.sl(t))
              stt("dve", Hb.sl(t), Hb.sl(t), rsc, GB.sl(), ALU.mult, ALU.mult)
              key = ("o", t)
              if key not in out_keys:
                  out_keys.append(key)
              r0 = ps_ * T + t * 128
              dma("sp", yout[r0:r0 + 128, :], Hb.full[:, t, :], reads=Hb.sl(t).toks, key=key)
    except _Stop:
        pass

    if seq_in is not None:
        P.emit(nc, final_wait_keys=out_keys)
    st.close()
    nc._memmap = memmap
    return nc, seq_rec


_CACHE = {}


def _get_nc():
    if "nc" not in _CACHE:
        _, seq = _build(None)
        nc, _ = _build(seq)
        _CACHE["nc"] = nc
    return _CACHE["nc"]


def _host_inputs(x, mem, g_mix, w_in, conv_w, attn_sinks, w_attn_proj, w_conv_proj, w_mix_out,
                 g_xattn, g_mem, w_xq, w_xkv, w_xo, g_ffn, w_ffn_in, w_ffn_out, g_final):
    f32 = np.float32
    x = np.asarray(x, f32)
    mem = np.asarray(mem, f32)
    perm = []
    for c in range(8):
        for r in range(2):
            hq = (2 * (c // 4) + r) * 4 + (c % 4)
            perm.extend(range(hq * 64, hq * 64 + 64))
    perm = np.asarray(perm)
    w_in0 = np.asarray(w_in, f32)[0]
    w_in_p = np.ascontiguousarray(np.concatenate([w_in0[:, perm], w_in0[:, 1024:]], axis=1))
    w_ap_p = np.ascontiguousarray(np.asarray(w_attn_proj, f32)[0][perm, :])
    gvec = np.ascontiguousarray(np.stack([np.asarray(g_mix, f32)[0], np.asarray(g_xattn, f32)[0],
                                          np.asarray(g_mem, f32)[0], np.asarray(g_ffn, f32)[0],
                                          np.asarray(g_final, f32)]))
    cw = np.asarray(conv_w, f32)[0]
    cwin = np.ascontiguousarray(cw.reshape(3, 8, 128).transpose(2, 1, 0).reshape(128, 24))
    sinkin = np.ascontiguousarray(np.asarray(attn_sinks, f32)[0].reshape(1, 16))
    ident = np.eye(128, dtype=f32).astype(ml_dtypes.bfloat16)
    jj = np.arange(128)[:, None]
    ii = np.arange(128)[None, :]
    gen = np.stack([(jj > ii), (jj <= ii)]).astype(f32)
    genb = ((gen - 1.0) * 30000.0).transpose(2, 0, 1)
    half = 32
    inv_freq = (f32(10000.0) ** (-np.arange(half, dtype=f32) / f32(half))).astype(f32)
    shared = {"gvec": gvec, "w_in": w_in_p, "w_ap": w_ap_p,
              "w_cp": np.ascontiguousarray(np.asarray(w_conv_proj, f32)[0]),
              "w_mix": np.ascontiguousarray(np.asarray(w_mix_out, f32)[0]),
              "w_xq": np.ascontiguousarray(np.asarray(w_xq, f32)[0]),
              "w_xkv": np.ascontiguousarray(np.asarray(w_xkv, f32)[0]),
              "w_xo": np.ascontiguousarray(np.asarray(w_xo, f32)[0]),
              "w_f1": np.ascontiguousarray(np.asarray(w_ffn_in, f32)[0]),
              "w_f2": np.ascontiguousarray(np.asarray(w_ffn_out, f32)[0]),
              "cwin": cwin, "sinkin": sinkin, "identin": ident}
    maps = []
    for c in range(NCORES):
        b = c // 2
        xin = np.zeros((NPASS, TH, D), f32)
        rope = np.zeros((NPASS, 128, 2, 9, 32), f32)
        masks = np.zeros((128, 3, 2, 128), f32)
        masks[:, 0] = genb
        for p in range(NPASS):
            start = (c % 2) * TOK_CORE + p * T
            if start > 0:
                xin[p, 0:128] = x[b, start - 128:start]
            xin[p, 128:] = x[b, start:start + T]
            pos = (start - 128 + np.arange(9 * 128)).astype(f32)
            ang = (pos[:, None] * inv_freq[None, :]).astype(f32)
            cs = np.cos(ang).astype(f32).reshape(9, 128, 32)
            sn = np.sin(ang).astype(f32).reshape(9, 128, 32)
            rope[p, :, 0] = cs.transpose(1, 0, 2)
            rope[p, :, 1] = sn.transpose(1, 0, 2)
            m = genb.copy()
            if start == 0:
                m[:, 0, :] = -30000.0
            masks[:, 1 + p] = m
        d = dict(shared)
        d["xin"] = xin
        d["memin"] = np.ascontiguousarray(mem[b])
        d["ropein"] = rope
        d["maskin"] = masks.astype(ml_dtypes.bfloat16)
        maps.append(d)
    return maps


def kernel(**inputs):
    maps = _host_inputs(**inputs)
    nc = _get_nc()
    res = run_bass_kernel_spmd(nc, maps, core_ids=list(range(NCORES)))
    out = np.empty((BATCH, SEQ, D), np.float32)
    for c in range(NCORES):
        b = c // 2
        s0 = (c % 2) * TOK_CORE
        out[b, s0:s0 + TOK_CORE] = res.results[c]["yout"]
    return out
```

```python
import numpy as np
import ml_dtypes
import concourse.bass as bass
import concourse.mybir as mybir
from concourse.bass_utils import run_bass_kernel_spmd

F32 = mybir.dt.float32
BF16 = mybir.dt.bfloat16
U8 = mybir.dt.uint8
AF = mybir.ActivationFunctionType
ALU = mybir.AluOpType

UNIT = 512


class _Op:
    __slots__ = ("eng", "fn", "deps_c", "deps_d", "dma", "sig", "seq", "idx")


class Prog:
    ENGS = ("pe", "act", "dve", "pool", "sp")

    def __init__(self):
        self.ops = []
        self.last_w = {}
        self.readers = {}
        self.dma_count = {}

    def op(self, eng, fn, reads=(), writes=(), dma=None):
        o = _Op()
        o.eng, o.fn, o.idx = eng, fn, len(self.ops)
        o.dma = None
        o.sig = False
        o.seq = 0
        deps = set()
        lw, rd = self.last_w, self.readers
        for t in reads:
            w = lw.get(t)
            if w is not None:
                deps.add(w)
        for t in writes:
            w = lw.get(t)
            if w is not None:
                deps.add(w)
            r = rd.get(t)
            if r:
                deps.update(r.values())
        for t in writes:
            lw[t] = o.idx
            rd[t] = {}
        for t in reads:
            r = rd.get(t)
            if r is None:
                r = rd[t] = {}
            r[eng if dma is None else ("dma", o.idx)] = o.idx
        deps.discard(o.idx)
        dc, dd = {}, {}
        for d in deps:
            p = self.ops[d]
            if p.dma is not None:
                k, c = p.dma
                if dd.get(k, 0) < c:
                    dd[k] = c
            else:
                if p.eng == "pe" and eng == "pe" and dma is None:
                    continue
                if dc.get(p.eng, -1) < d:
                    dc[p.eng] = d
        o.deps_c, o.deps_d = dc, dd
        if dma is not None:
            c = self.dma_count.get(dma, 0) + 1
            self.dma_count[dma] = c
            o.dma = (dma, 16 * c)
        self.ops.append(o)
        return o

    def emit(self, nc, final_wait_keys=()):
        ops = self.ops
        for o in ops:
            for d in o.deps_c.values():
                ops[d].sig = True
        cnt = {e: 0 for e in self.ENGS}
        for o in ops:
            if o.dma is None and o.sig:
                cnt[o.eng] += 1
                o.seq = cnt[o.eng]
        from contextlib import ExitStack
        with ExitStack() as st:
            esem = {e: st.enter_context(nc.semaphore("s_" + e)) for e in self.ENGS}
            dsem = {k: st.enter_context(nc.semaphore("d_%d" % i)) for i, k in enumerate(self.dma_count)}
            block = st.enter_context(nc.Block())
            per = {e: [o for o in ops if o.eng == e] for e in self.ENGS}

            def body(ename):
                def run(e):
                    wc = {x: 0 for x in self.ENGS}
                    wd = {}
                    for o in per[ename]:
                        for pe_, d in o.deps_c.items():
                            need = ops[d].seq
                            if wc[pe_] < need:
                                e.wait_ge(esem[pe_], need)
                                wc[pe_] = need
                        for k, c in o.deps_d.items():
                            if wd.get(k, 0) < c:
                                e.wait_ge(dsem[k], c)
                                wd[k] = c
                        ins = o.fn(e)
                        if o.dma is not None:
                            ins.then_inc(dsem[o.dma[0]], 16)
                        elif o.sig:
                            ins.then_inc(esem[ename], 1)
                    if ename == "sp":
                        for k in final_wait_keys:
                            e.wait_ge(dsem[k], 16 * self.dma_count[k])
                return run

            block.tensor(body("pe"))
            block.scalar(body("act"))
            block.vector(body("dve"))
            block.gpsimd(body("pool"))
            block.sync(body("sp"))


D = 2048
SEQ = 4096
BATCH = 4
NCORES = 8
TOK_CORE = 2048
NPASS = 2
T = 1024
NT = 8
TH = T + 128
FFN = 5632
EPS = 1e-6
NSLOT = 4
SLOT_BYTES = 8192


class Ref:
    __slots__ = ("ap", "toks")

    def __init__(self, ap, toks):
        self.ap, self.toks = ap, toks


class Buf:
    def __init__(self, sb, off, dt, shape):
        self.sb, self.off, self.dt, self.shape = sb, off, dt, tuple(shape)
        self.esz = 2 if dt == BF16 else 4
        n = 1
        for s in shape:
            n *= s
        self.n = n
        self.nbytes = n * self.esz
        full = sb[:, off:off + self.nbytes].bitcast(dt)
        if len(shape) == 2:
            full = full.rearrange("p (a b) -> p a b", a=shape[0])
        elif len(shape) == 3:
            full = full.rearrange("p (a b c) -> p a b c", a=shape[0], b=shape[1])
        self.full = full

    def sl(self, *idx, p=(0, 128)):
        shape = self.shape
        idx = list(idx) + [slice(None)] * (len(shape) - len(idx))
        rng = []
        for i, s in zip(idx, shape):
            if isinstance(i, int):
                rng.append((i, i + 1))
            else:
                a = 0 if i.start is None else i.start
                b = s if i.stop is None else i.stop
                rng.append((a, b))
        ap = self.full[(slice(p[0], p[1]),) + tuple(idx)]
        strides = [1] * len(shape)
        for d in range(len(shape) - 2, -1, -1):
            strides[d] = strides[d + 1] * shape[d + 1]
        toks = set()
        lead = rng[:-1]
        la, lb = rng[-1]

        def rec(d, base):
            if d == len(lead):
                b0 = self.off + (base + la) * self.esz
                b1 = self.off + (base + lb) * self.esz - 1
                for u in range(b0 // UNIT, b1 // UNIT + 1):
                    toks.add(u)
                return
            for i in range(lead[d][0], lead[d][1]):
                rec(d + 1, base + i * strides[d])

        rec(0, 0)
        return Ref(ap, list(toks))


class _Stop(Exception):
    pass


def _build(seq_in=None, stop=None):
    nc = bass.Bass("TRN2", target_bir_lowering=False)
    P = Prog()
    dram = {}

    def din(name, shape, dt=F32):
        dram[name] = nc.dram_tensor(name, list(shape), dt, kind="ExternalInput").ap()
        return dram[name]

    xin = din("xin", [NPASS, TH, D])
    memin = din("memin", [256, D])
    gvec = din("gvec", [5, D])
    w_in = din("w_in", [D, 8704])
    w_ap = din("w_ap", [1024, D])
    w_cp = din("w_cp", [1024, D])
    w_mix = din("w_mix", [D, D])
    w_xq = din("w_xq", [D, 512])
    w_xkv = din("w_xkv", [D, 1024])
    w_xo = din("w_xo", [512, D])
    w_f1 = din("w_f1", [D, 2 * FFN])
    w_f2 = din("w_f2", [FFN, D])
    cwin = din("cwin", [128, 24])
    sinkin = din("sinkin", [1, 16])
    ropein = din("ropein", [NPASS, 128, 2, 9, 32])
    maskin = din("maskin", [128, 3, 2, 128], BF16)
    identin = din("identin", [128, 128], BF16)
    yout = nc.dram_tensor("yout", [TOK_CORE, D], F32, kind="ExternalOutput").ap()
    dbg = nc.dram_tensor("dbg", [128, 206 * 1024], U8, kind="ExternalOutput").ap() if stop else None
    memmap = {}
    wd = {"w_in": w_in, "w_ap": w_ap, "w_cp": w_cp, "w_mix": w_mix, "w_xq": w_xq, "w_xkv": w_xkv,
          "w_xo": w_xo, "w_f1": w_f1, "w_f2": w_f2}
    wview = {k: v.rearrange("(k p) c -> p k c", p=128) for k, v in wd.items()}

    from contextlib import ExitStack
    st = ExitStack()
    SBYTES = 206 * 1024
    sb = st.enter_context(nc.sbuf_tensor("sb", [128, SBYTES], U8))
    ps = st.enter_context(nc.psum_tensor("ps", [128, 8, 512], F32))

    off = [0]

    def alloc(nbytes, align=64):
        o = (off[0] + align - 1) // align * align
        off[0] = o + nbytes
        assert off[0] <= SBYTES, ("SBUF overflow", off[0])
        return o

    O_H = alloc(NT * 8192, 512)
    O_UT = alloc(16 * TH * 2, 512)
    O_R3 = alloc(32768, 512)
    O_WS = alloc(NSLOT * SLOT_BYTES, 512)
    O_SCR = alloc(16384, 512)
    O_GB = alloc(8192, 512)
    O_UB = alloc(4096, 512)
    O_ROPE = alloc(2 * 9 * 32 * 4, 512)
    O_MASK = alloc(3 * 2 * 128 * 2, 512)
    O_ID = alloc(256, 512)
    O_ONES = alloc(256, 512)
    O_MEMK = alloc(4 * 256 * 2, 512)
    O_MEMV = alloc(2 * 512 * 2, 512)
    O_SS = alloc(80 * 4, 512)
    O_RS = alloc(80 * 4, 512)
    O_CW = alloc(24 * 4, 512)
    O_SINK = alloc(16 * 4, 64)
    O_SK2 = alloc(2 * 4 * 4, 64)
    O_I4 = alloc(1024, 512)
    O_ONESP = alloc(512, 512)

    Hb = Buf(sb, O_H, F32, [NT, D])
    UT = Buf(sb, O_UT, BF16, [16, TH])
    GB = Buf(sb, O_GB, F32, [D])
    UB = Buf(sb, O_UB, BF16, [D])
    ROPE = Buf(sb, O_ROPE, F32, [2, 9, 32])
    MASK = Buf(sb, O_MASK, BF16, [3, 2, 128])
    IDT = Buf(sb, O_ID, BF16, [128])
    ONES = Buf(sb, O_ONES, BF16, [128])
    MEMK = Buf(sb, O_MEMK, BF16, [4, 256])
    MEMV = Buf(sb, O_MEMV, BF16, [2, 512])
    SS = Buf(sb, O_SS, F32, [80])
    RS = Buf(sb, O_RS, F32, [80])
    CW = Buf(sb, O_CW, F32, [24])
    SINK = Buf(sb, O_SINK, F32, [16])
    SK2 = Buf(sb, O_SK2, F32, [2, 4])
    I4 = Buf(sb, O_I4, BF16, [512])
    ONESP = Buf(sb, O_ONESP, BF16, [2, 128])
    AO = Buf(sb, O_H + 4 * 8192, BF16, [8, T])
    CV = Buf(sb, O_H + 6 * 8192, BF16, [8, T])
    QT = Buf(sb, O_R3, BF16, [8, T])
    KT = Buf(sb, O_R3 + 16384, BF16, [2, TH])
    VA = Buf(sb, O_R3 + 16384 + 4608, BF16, [9, 4, 128])
    MG = Buf(sb, O_R3, BF16, [16, T])
    XQ = Buf(sb, O_R3, BF16, [4, T])
    XO = Buf(sb, O_R3 + 8192, BF16, [4, T])
    ACTb = [Buf(sb, O_R3 + i * 16384, BF16, [8, T]) for i in range(2)]
    XH = Buf(sb, O_SCR, F32, [D])
    JUNK = Buf(sb, O_SCR + 8192, BF16, [D])

    def psr(b, c0=0, c1=512, p=(0, 128)):
        return Ref(ps[p[0]:p[1], b, c0:c1], [("ps", b)])

    def psr2(b, p=(0, 128)):
        return Ref(ps[p[0]:p[1], b:b + 2, :].rearrange("p a b -> p (a b)"), [("ps", b), ("ps", b + 1)])

    def psbf(b, c0, c1):
        return Ref(ps[:, b, :].bitcast(BF16)[:, c0:c1], [("ps", b)])

    bank_ctr = [0]

    def banks(n=1):
        b = bank_ctr[0]
        if n == 2 and b % 2:
            b += 1
        if b + n > 8:
            b = 0
        bank_ctr[0] = (b + n) % 8
        return b

    def mm(out, lhsT, rhs, start, stop):
        P.op("pe", lambda e: e.matmul(out.ap, lhsT=lhsT.ap, rhs=rhs.ap, start=start, stop=stop),
             reads=lhsT.toks + rhs.toks, writes=out.toks)

    def tr(out, in_):
        P.op("pe", lambda e: e.transpose(out=out.ap, in_=in_.ap, identity=IDT.full),
             reads=in_.toks + IDT.sl().toks, writes=out.toks)

    def act(out, in_, func, scale=1.0, bias=0.0, accum=None):
        sc = scale.ap if isinstance(scale, Ref) else scale
        rd = in_.toks + (scale.toks if isinstance(scale, Ref) else [])
        wr = out.toks + (accum.toks if accum is not None else [])
        if accum is None:
            P.op("act", lambda e: e.activation(out=out.ap, in_=in_.ap, func=func, bias=bias, scale=sc),
                 reads=rd, writes=wr)
        else:
            P.op("act", lambda e: e.activation(out=out.ap, in_=in_.ap, func=func, bias=bias, scale=sc,
                                               accum_out=accum.ap), reads=rd, writes=wr)

    def cp(eng, out, in_):
        if eng == "act":
            P.op("act", lambda e: e.copy(out=out.ap, in_=in_.ap), reads=in_.toks, writes=out.toks)
        else:
            P.op(eng, lambda e: e.tensor_copy(out=out.ap, in_=in_.ap), reads=in_.toks, writes=out.toks)

    def tt(eng, out, in0, in1, op):
        P.op(eng, lambda e: e.tensor_tensor(out=out.ap, in0=in0.ap, in1=in1.ap, op=op),
             reads=in0.toks + in1.toks, writes=out.toks)

    def ts(eng, out, in0, s1, s2, op0, op1=None):
        a1 = s1.ap if isinstance(s1, Ref) else s1
        a2 = s2.ap if isinstance(s2, Ref) else s2
        rd = in0.toks + (s1.toks if isinstance(s1, Ref) else []) + (s2.toks if isinstance(s2, Ref) else [])
        if op1 is None:
            P.op(eng, lambda e: e.tensor_scalar(out=out.ap, in0=in0.ap, scalar1=a1, scalar2=None, op0=op0),
                 reads=rd, writes=out.toks)
        else:
            P.op(eng, lambda e: e.tensor_scalar(out=out.ap, in0=in0.ap, scalar1=a1, scalar2=a2, op0=op0, op1=op1),
                 reads=rd, writes=out.toks)

    def stt(eng, out, in0, scalar, in1, op0, op1):
        a = scalar.ap if isinstance(scalar, Ref) else scalar
        rd = in0.toks + in1.toks + (scalar.toks if isinstance(scalar, Ref) else [])
        P.op(eng, lambda e: e.scalar_tensor_tensor(out=out.ap, in0=in0.ap, scalar=a, in1=in1.ap, op0=op0, op1=op1),
             reads=rd, writes=out.toks)

    def memset(eng, out, val):
        P.op(eng, lambda e: e.memset(out.ap, val), writes=out.toks)

    dma_ctr = [0]

    def dma(eng, out_ap, in_ap, reads=(), writes=(), key=None):
        if key is None:
            dma_ctr[0] += 1
            key = ("m", dma_ctr[0] % 6)
        P.op(eng, lambda e: e.dma_start(out=out_ap, in_=in_ap), reads=list(reads), writes=list(writes), dma=key)

    RING = NSLOT * SLOT_BYTES
    seq_rec = []
    ws = {"get": 0, "emit": 0, "rel": 0}
    ws_offs, ws_need = [], []
    if seq_in is not None:
        head = 0
        for (_n, _k0, nk, _c0, ncols) in seq_in:
            size = nk * ncols * 2
            o = (head + size - 1) // size * size
            if o + size > RING:
                o = 0
            ws_offs.append(o)
            head = o + size
        for j, dj in enumerate(seq_in):
            sj = dj[2] * dj[4] * 2
            nd = -1
            for i in range(j - 1, max(-1, j - 40), -1):
                si = seq_in[i][2] * seq_in[i][4] * 2
                if ws_offs[i] < ws_offs[j] + sj and ws_offs[j] < ws_offs[i] + si:
                    nd = i
                    break
            ws_need.append(nd)

    def ws_dma(name, k0, nk, c0, ncols, o):
        b = Buf(sb, O_WS + o, BF16, [nk, ncols])
        r = b.sl()
        src = wview[name][:, k0:k0 + nk, c0:c0 + ncols]
        P.op("pool", lambda e: e.dma_start(out=r.ap, in_=src), writes=r.toks, dma=("ws", o // 2048))
        return b

    def ws_pump():
        while ws["emit"] < len(seq_in) and ws_need[ws["emit"]] < ws["rel"]:
            j = ws["emit"]
            ws_dma(*seq_in[j], ws_offs[j])
            ws["emit"] = j + 1

    def ws_get(name, k0, nk, c0, ncols):
        assert nk * ncols * 2 <= SLOT_BYTES
        d = (name, k0, nk, c0, ncols)
        seq_rec.append(d)
        j = ws["get"]
        ws["get"] = j + 1
        if seq_in is None:
            return ws_dma(*d, 0)
        assert seq_in[j] == d, (j, seq_in[j], d)
        ws_pump()
        assert ws["emit"] > j, ("weight ring too small for slabs held at once", j)
        return Buf(sb, O_WS + ws_offs[j], BF16, [nk, ncols])

    def ws_done(n=1):
        ws["rel"] += n
        if seq_in is not None:
            ws_pump()

    dma("sp", IDT.full, identin, writes=IDT.sl().toks, key="c0")
    dma("sp", MASK.full, maskin, writes=MASK.sl().toks, key="c1")
    dma("sp", CW.full, cwin, writes=CW.sl().toks, key="c2")
    dma("sp", SINK.full, sinkin[0:1, :].broadcast_to([128, 16]), writes=SINK.sl().toks, key="c3")
    memset("dve", ONES.sl(), 1.0)
    memset("dve", SS.sl(), 0.0)
    act(SINK.sl(), SINK.sl(), AF.Exp)
    for kc_ in range(2):
        for hh_ in range(2):
            pr_ = (64 * hh_, 64 * hh_ + 64)
            hk_ = 2 * kc_ + hh_
            cp("dve", SK2.sl(kc_, p=pr_), SINK.sl(slice(hk_ * 4, hk_ * 4 + 4), p=pr_))
    for g_ in range(4):
        cp("dve", I4.sl(slice(g_ * 128, g_ * 128 + 128)), IDT.sl())
    memset("dve", ONESP.sl(), 0.0)
    memset("dve", ONESP.sl(0, slice(0, 64)), 1.0)
    memset("dve", ONESP.sl(1, slice(64, 128)), 1.0)
    stat_ctr = [0]

    def load_g(i):
        dma("sp", GB.full, gvec[i:i + 1, :].broadcast_to([128, D]), writes=GB.sl().toks, key="g")

    UB2 = Buf(sb, O_SCR + 12288, BF16, [D])

    def norm_rstd(xts):
        n = len(xts)
        c0 = stat_ctr[0]
        stat_ctr[0] += n
        for i, xt in enumerate(xts):
            act(JUNK.sl(), xt, AF.Square, accum=SS.sl(slice(c0 + i, c0 + i + 1)))
        blk = RS.sl(slice(c0, c0 + n))
        ts("dve", blk, SS.sl(slice(c0, c0 + n)), 1.0 / D, EPS, ALU.mult, ALU.add)
        act(blk, blk, AF.Sqrt)
        P.op("dve", lambda e: e.reciprocal(out=blk.ap, in_=blk.ap), reads=blk.toks, writes=blk.toks)
        return [RS.sl(slice(c0 + i, c0 + i + 1)) for i in range(n)]

    def norm_rstd1(xt):
        c = stat_ctr[0]
        stat_ctr[0] += 1
        ssc = SS.sl(slice(c, c + 1))
        rsc = RS.sl(slice(c, c + 1))
        act(JUNK.sl(), xt, AF.Square, accum=ssc)
        ts("dve", rsc, ssc, 1.0 / D, EPS, ALU.mult, ALU.add)
        act(rsc, rsc, AF.Sqrt)
        P.op("dve", lambda e: e.reciprocal(out=rsc.ap, in_=rsc.ap), reads=rsc.toks, writes=rsc.toks)
        return rsc

    def norm_phase(xts, cols):
        rs = norm_rstd(xts)
        for i, (xt, col0) in enumerate(zip(xts, cols)):
            ub = UB if i % 2 == 0 else UB2
            stt("dve", ub.sl(), xt, rs[i], GB.sl(), ALU.mult, ALU.mult)
            for half in range(2):
                b = banks(1)
                for j in range(8):
                    k = half * 8 + j
                    tr(psbf(b, j * 128, (j + 1) * 128), ub.sl(slice(k * 128, (k + 1) * 128)))
                src = Ref(ps[:, b, :].bitcast(BF16).rearrange("p (a b) -> p a b", a=8), [("ps", b)])
                cp("act", UT.sl(slice(half * 8, half * 8 + 8), slice(col0, col0 + 128)), src)

    load_g(2)
    for mt in range(2):
        dma("sp", XH.full, memin[mt * 128:(mt + 1) * 128, :], writes=XH.sl().toks, key="xh")
        norm_phase([XH.sl()], [mt * 128])
    for s in range(2):
        slab = ws_get("w_xkv", 0, 16, s * 256, 256)
        for hh in range(2):
            hx = s * 2 + hh
            b = banks(1)
            for k in range(16):
                mm(psr(b, 0, 256), slab.sl(k, slice(hh * 128, hh * 128 + 128)), UT.sl(k, slice(0, 256)), k == 0, k == 15)
            cp("act", MEMK.sl(hx), psr(b, 0, 256))
        ws_done()
    for s in range(2):
        slab = ws_get("w_xkv", 0, 16, 512 + s * 256, 256)
        for mt in range(2):
            b = banks(1)
            for k in range(16):
                mm(psr(b, 0, 256), UT.sl(k, slice(mt * 128, mt * 128 + 128)), slab.sl(k), k == 0, k == 15)
            cp("dve", MEMV.sl(mt, slice(s * 256, s * 256 + 256)), psr(b, 0, 256))
        ws_done()

    out_keys = []

    def chk(name, ps_):
        if stop == (name, ps_):
            allt = list(range(0, SBYTES // UNIT))
            dma("sp", dbg, sb[:, :], reads=allt, key="dbg")
            out_keys.append("dbg")
            raise _Stop()

    memmap.update(dict(O_H=O_H, O_UT=O_UT, O_R3=O_R3, O_SCR=O_SCR, O_MEMK=O_MEMK, O_MEMV=O_MEMV, O_RS=O_RS, O_SS=O_SS,
                       O_SINK=O_SINK, O_GB=O_GB, O_UB=O_UB, O_ROPE=O_ROPE, O_MASK=O_MASK))
    try:
      for ps_ in range(NPASS):
          dma("sp", ROPE.full, ropein[ps_], writes=ROPE.sl().toks, key="rope")
          load_g(0)
          dma("sp", XH.full, xin[ps_, 0:128, :], writes=XH.sl().toks, key="xh")
          for t in range(NT):
              dma("sp", Hb.full[:, t, :], xin[ps_, 128 + t * 128:256 + t * 128, :], writes=Hb.sl(t).toks, key=("x", t))
          norm_phase([XH.sl()] + [Hb.sl(t) for t in range(NT)], [128 * t for t in range(NT + 1)])

          chk("A", ps_)
          memset("dve", VA.sl(), 0.0)
          SC_A = Buf(sb, O_SCR, F32, [2, 256])
          SC_B = Buf(sb, O_SCR + 2048, F32, [2, 256])
          SC_Q = Buf(sb, O_SCR + 4096, BF16, [2, 256])
          it = 0
          pend = []
          for s in range(6):
              slab = ws_get("w_in", 0, 16, s * 256, 256)
              for t in range(9):
                  if s < 4 and t == 0:
                      continue
                  b = banks(1)
                  for k in range(16):
                      mm(psr(b, 0, 256), UT.sl(k, slice(t * 128, t * 128 + 128)), slab.sl(k), k == 0, k == 15)
                  if s == 5:
                      pv = ps[:, b, 0:256].rearrange("p (a b c) -> p a b c", a=2, b=2)
                      va = VA.full[:, t, :, :].rearrange("p (a b) c -> p a b c", a=2)
                      P.op("act", lambda e, pv=pv, va=va: e.copy(out=va[:, :, 0, 0:64], in_=pv[:, :, 0, :]),
                           reads=[("ps", b)], writes=VA.sl(t).toks)
                      P.op("dve", lambda e, pv=pv, va=va: e.tensor_copy(out=va[:, :, 1, 64:128], in_=pv[:, :, 1, :]),
                           reads=[("ps", b)], writes=VA.sl(t).toks)
                      continue
                  i2 = it % 2
                  it += 1
                  A_ = SC_A.sl(i2)
                  B_ = SC_B.sl(i2)
                  Q_ = SC_Q.sl(i2)
                  pq = ps[:, b, 0:256].rearrange("p (h two f) -> p h two f", h=4, two=2)
                  cosb = ROPE.full[:, 0, t, :].unsqueeze(1).unsqueeze(1).broadcast_to([128, 4, 2, 32])
                  sinb = ROPE.full[:, 1, t, :].unsqueeze(1).broadcast_to([128, 4, 32])
                  Aap = SC_A.full[:, i2, :].rearrange("p (h two f) -> p h two f", h=4, two=2)
                  Bap = SC_B.full[:, i2, :].rearrange("p (h two f) -> p h two f", h=4, two=2)
                  rt = ROPE.sl().toks
                  P.op("dve", lambda e, Aap=Aap, pq=pq, cosb=cosb: e.tensor_tensor(out=Aap, in0=pq, in1=cosb, op=ALU.mult),
                       reads=[("ps", b)] + rt, writes=A_.toks)
                  P.op("dve", lambda e, Bap=Bap, pq=pq, sinb=sinb: e.scalar_tensor_tensor(
                      out=Bap[:, :, 0, :], in0=pq[:, :, 1, :], scalar=-1.0, in1=sinb, op0=ALU.mult, op1=ALU.mult),
                      reads=[("ps", b)] + rt, writes=B_.toks)
                  P.op("dve", lambda e, Bap=Bap, pq=pq, sinb=sinb: e.tensor_tensor(
                      out=Bap[:, :, 1, :], in0=pq[:, :, 0, :], in1=sinb, op=ALU.mult),
                      reads=[("ps", b)] + rt, writes=B_.toks)
                  tt("dve", Q_, A_, B_, ALU.add)

                  def fin(s=s, t=t, i2=i2):
                      b2 = banks(1)
                      for j in range(2):
                          tr(psbf(b2, j * 128, (j + 1) * 128), SC_Q.sl(i2, slice(j * 128, (j + 1) * 128)))
                      src = Ref(ps[:, b2, :].bitcast(BF16)[:, 0:256].rearrange("p (a b) -> p a b", a=2), [("ps", b2)])
                      if s < 4:
                          cp("act", QT.sl(slice(2 * s, 2 * s + 2), slice((t - 1) * 128, t * 128)), src)
                      else:
                          cp("act", KT.sl(slice(0, 2), slice(t * 128, (t + 1) * 128)), src)
                  pend.append(fin)
                  if len(pend) > 1:
                      pend.pop(0)()
              ws_done()
          while pend:
              pend.pop(0)()

          chk("B", ps_)
          PT = [Buf(sb, O_SCR + i * 4096, BF16, [2, 2, 512]) for i in range(2)]
          RC = [Buf(sb, O_SCR + 8192 + i * 2048, F32, [512]) for i in range(2)]
          iters = [(n, kc) for n in range(NT) for kc in range(2)]

          def qk(n, kc, i2):
            mi = (1 + ps_) if n == 0 else 0
            for hh in range(2):
                pr = (64 * hh, 64 * hh + 64)
                for j in range(2):
                    bk = 2 * hh + j
                    mm(psr(bk), KT.sl(kc, slice((n + j) * 128, (n + j + 1) * 128), p=pr),
                       QT.sl(slice(4 * kc, 4 * kc + 4), slice(n * 128, (n + 1) * 128), p=pr), True, False)
                    mm(psr(bk), MASK.sl(mi, j), I4.sl(), False, True)
                act(PT[i2].sl(hh), Ref(ps[:, 2 * hh:2 * hh + 2, :], [("ps", 2 * hh), ("ps", 2 * hh + 1)]), AF.Exp, scale=0.125)

          def pv(n, kc, i2):
            bo = 4 + 2 * i2
            for q_ in range(4):
                hh, j = q_ // 2, q_ % 2
                mm(psr(bo), VA.sl(n + j, 2 * kc + hh), PT[i2].sl(hh, j), q_ == 0, q_ == 3)
            for q_ in range(4):
                hh, j = q_ // 2, q_ % 2
                mm(psr(bo + 1), ONESP.sl(hh), PT[i2].sl(hh, j), q_ == 0, q_ == 3)
            skb = SK2.full[:, kc, :].unsqueeze(2).broadcast_to([128, 4, 128])
            rc = RC[i2].full.rearrange("p (g q) -> p g q", g=4)
            pd = ps[:, bo + 1, :].rearrange("p (g q) -> p g q", g=4)
            P.op("dve", lambda e: e.tensor_tensor(out=rc, in0=pd, in1=skb, op=ALU.add),
                 reads=[("ps", bo + 1)] + SK2.sl().toks, writes=RC[i2].sl().toks)
            act(RC[i2].sl(), RC[i2].sl(), AF.Ln)
            act(RC[i2].sl(), RC[i2].sl(), AF.Exp, scale=-1.0)
            po = ps[:, bo, :].rearrange("p (g q) -> p g q", g=4)
            ao = AO.sl(slice(4 * kc, 4 * kc + 4), slice(n * 128, (n + 1) * 128))
            P.op("dve", lambda e: e.tensor_tensor(out=ao.ap, in0=po, in1=rc, op=ALU.mult),
                 reads=[("ps", bo)] + RC[i2].sl().toks, writes=ao.toks)

          for i, (n, kc) in enumerate(iters):
              qk(n, kc, i % 2)
              if i > 0:
                  pv(*iters[i - 1], (i - 1) % 2)
          pv(*iters[-1], (len(iters) - 1) % 2)
          bank_ctr[0] = 0

          chk("C", ps_)
          ZS = Buf(sb, O_SCR, F32, [1026])
          CZ = Buf(sb, O_SCR + 4608, F32, [1026])
          AC = Buf(sb, O_SCR + 9216, F32, [1024])
          for c in range(8):
              wcol = slice(0, 128)
              slz = ws_get("w_in", 0, 16, 1536 + c * 128, 128)
              bz = banks(2)
              bzh = banks(1)
              for k in range(16):
                  w = slz.sl(k, wcol)
                  mm(psr(bz), w, UT.sl(k, slice(128, 640)), k == 0, k == 15)
                  mm(psr(bz + 1), w, UT.sl(k, slice(640, 1152)), k == 0, k == 15)
                  mm(psr(bzh, 0, 2), w, UT.sl(k, slice(126, 128)), k == 0, k == 15)
              ws_done()
              cp("act", ZS.sl(slice(2, 1026)), psr2(bz))
              cp("act", ZS.sl(slice(0, 2)), psr(bzh, 0, 2))
              slg = ws_get("w_in", 0, 16, 3584 + c * 128, 128)
              bg = banks(2)
              bgh = banks(1)
              for k in range(16):
                  w = slg.sl(k, wcol)
                  mm(psr(bg), w, UT.sl(k, slice(128, 640)), k == 0, k == 15)
                  mm(psr(bg + 1), w, UT.sl(k, slice(640, 1152)), k == 0, k == 15)
                  mm(psr(bgh, 0, 2), w, UT.sl(k, slice(126, 128)), k == 0, k == 15)
              ws_done()
              tt("dve", CZ.sl(slice(2, 1026)), psr2(bg), ZS.sl(slice(2, 1026)), ALU.mult)
              tt("dve", CZ.sl(slice(0, 2)), psr(bgh, 0, 2), ZS.sl(slice(0, 2)), ALU.mult)
              slb = ws_get("w_in", 0, 16, 2560 + c * 128, 128)
              bb = banks(2)
              for k in range(16):
                  w = slb.sl(k, wcol)
                  mm(psr(bb), w, UT.sl(k, slice(128, 640)), k == 0, k == 15)
                  mm(psr(bb + 1), w, UT.sl(k, slice(640, 1152)), k == 0, k == 15)
              ws_done()
              act(AC.sl(), CZ.sl(slice(0, 1024)), AF.Copy, scale=CW.sl(slice(c * 3, c * 3 + 1)))
              stt("dve", AC.sl(), CZ.sl(slice(1, 1025)), CW.sl(slice(c * 3 + 1, c * 3 + 2)), AC.sl(), ALU.mult, ALU.add)
              stt("dve", AC.sl(), CZ.sl(slice(2, 1026)), CW.sl(slice(c * 3 + 2, c * 3 + 3)), AC.sl(), ALU.mult, ALU.add)
              tt("dve", CV.sl(c), psr2(bb), AC.sl(), ALU.mult)

          chk("D", ps_)
          SGA = Buf(sb, O_SCR, F32, [1024])
          M1 = Buf(sb, O_SCR + 4096, F32, [1024])
          SGC = Buf(sb, O_SCR + 8192, F32, [1024])
          for f in range(16):
              wcol = slice(0, 128)
              sla = ws_get("w_in", 0, 16, 4608 + f * 128, 128)
              b = banks(2)
              for k in range(16):
                  w = sla.sl(k, wcol)
                  mm(psr(b), w, UT.sl(k, slice(128, 640)), k == 0, k == 15)
                  mm(psr(b + 1), w, UT.sl(k, slice(640, 1152)), k == 0, k == 15)
              ws_done()
              act(SGA.sl(), psr2(b), AF.Sigmoid)
              slp = ws_get("w_ap", 0, 8, f * 128, 128)
              b = banks(2)
              for k in range(8):
                  w = slp.sl(k, wcol)
                  mm(psr(b), w, AO.sl(k, slice(0, 512)), k == 0, k == 7)
                  mm(psr(b + 1), w, AO.sl(k, slice(512, 1024)), k == 0, k == 7)
              ws_done()
              tt("dve", M1.sl(), psr2(b), SGA.sl(), ALU.mult)
              slc = ws_get("w_in", 0, 16, 6656 + f * 128, 128)
              b = banks(2)
              for k in range(16):
                  w = slc.sl(k, wcol)
                  mm(psr(b), w, UT.sl(k, slice(128, 640)), k == 0, k == 15)
                  mm(psr(b + 1), w, UT.sl(k, slice(640, 1152)), k == 0, k == 15)
              ws_done()
              act(SGC.sl(), psr2(b), AF.Sigmoid)
              slq = ws_get("w_cp", 0, 8, f * 128, 128)
              b = banks(2)
              for k in range(8):
                  w = slq.sl(k, wcol)
                  mm(psr(b), w, CV.sl(k, slice(0, 512)), k == 0, k == 7)
                  mm(psr(b + 1), w, CV.sl(k, slice(512, 1024)), k == 0, k == 7)
              ws_done()
              tt("dve", SGC.sl(), psr2(b), SGC.sl(), ALU.mult)
              tt("dve", MG.sl(f), M1.sl(), SGC.sl(), ALU.add)

          chk("E", ps_)
          for t in range(4, NT):
              dma("sp", Hb.full[:, t, :], xin[ps_, 128 + t * 128:256 + t * 128, :], writes=Hb.sl(t).toks, key=("x", t))
          load_g(1)
          for nb in range(4):
              for kh in range(2):
                  slab = ws_get("w_mix", kh * 8, 8, nb * 512, 512)
                  for t in range(NT):
                      b = banks(1)
                      for k in range(8):
                          mm(psr(b), MG.sl(kh * 8 + k, slice(t * 128, t * 128 + 128)), slab.sl(k), k == 0, k == 7)
                      hs = Hb.sl(t, slice(nb * 512, nb * 512 + 512))
                      tt("dve", hs, psr(b), hs, ALU.add)
                  ws_done()

          chk("F", ps_)
          norm_phase([Hb.sl(t) for t in range(NT)], [128 + 128 * t for t in range(NT)])
          for s in range(2):
              slab = ws_get("w_xq", 0, 16, s * 256, 256)
              for hh in range(2):
                  hx = 2 * s + hh
                  b = banks(2)
                  for k in range(16):
                      w = slab.sl(k, slice(hh * 128, hh * 128 + 128))
                      mm(psr(b), w, UT.sl(k, slice(128, 640)), k == 0, k == 15)
                      mm(psr(b + 1), w, UT.sl(k, slice(640, 1152)), k == 0, k == 15)
                  cp("act", XQ.sl(hx), psr2(b))
              ws_done()
          PX = [Buf(sb, O_SCR + i * 2048, BF16, [2, 512]) for i in range(2)]
          RX = [Buf(sb, O_SCR + 4096 + i * 2048, F32, [512]) for i in range(2)]
          xit = [(hx, th) for hx in range(4) for th in range(2)]

          def xqk(i):
              hx, th = xit[i]
              i2 = i % 2
              b = banks(2)
              for mt in range(2):
                  mm(psr(b + mt), MEMK.sl(hx, slice(mt * 128, mt * 128 + 128)), XQ.sl(hx, slice(th * 512, th * 512 + 512)), True, True)
              act(PX[i2].sl(), Ref(ps[:, b:b + 2, :], [("ps", b), ("ps", b + 1)]), AF.Exp, scale=float(128 ** -0.5))

          def xpv(i):
              hx, th = xit[i]
              i2 = i % 2
              bo = banks(2)
              for mt in range(2):
                  mm(psr(bo), MEMV.sl(mt, slice(hx * 128, hx * 128 + 128)), PX[i2].sl(mt), mt == 0, mt == 1)
              for mt in range(2):
                  mm(psr(bo + 1), ONES.sl(), PX[i2].sl(mt), mt == 0, mt == 1)
              act(RX[i2].sl(), psr(bo + 1), AF.Ln)
              act(RX[i2].sl(), RX[i2].sl(), AF.Exp, scale=-1.0)
              tt("dve", XO.sl(hx, slice(th * 512, th * 512 + 512)), psr(bo), RX[i2].sl(), ALU.mult)

          xqk(0)
          for i in range(len(xit)):
              if i + 1 < len(xit):
                  xqk(i + 1)
              xpv(i)
          load_g(3)
          xslabs = [ws_get("w_xo", 0, 4, nb * 512, 512) for nb in range(4)]
          for t in range(NT):
              for nb in range(4):
                  b = banks(1)
                  for k in range(4):
                      mm(psr(b), XO.sl(k, slice(t * 128, t * 128 + 128)), xslabs[nb].sl(k), k == 0, k == 3)
                  hs = Hb.sl(t, slice(nb * 512, nb * 512 + 512))
                  tt("dve", hs, psr(b), hs, ALU.add)
          ws_done(4)

          chk("G", ps_)
          norm_phase([Hb.sl(t) for t in range(NT)], [128 + 128 * t for t in range(NT)])
          SG = [Buf(sb, O_SCR + i * 4096, F32, [1024]) for i in range(2)]
          groups = [8, 8, 8, 8, 8, 4]
          j0 = 0
          it = 0
          for gi, G in enumerate(groups):
              AB = ACTb[gi % 2]
              for jl in range(G):
                  jj = j0 + jl
                  wcol = slice(0, 128)
                  i2 = it % 2
                  it += 1
                  slg = ws_get("w_f1", 0, 16, jj * 128, 128)
                  b = banks(2)
                  for k in range(16):
                      w = slg.sl(k, wcol)
                      mm(psr(b), w, UT.sl(k, slice(128, 640)), k == 0, k == 15)
                      mm(psr(b + 1), w, UT.sl(k, slice(640, 1152)), k == 0, k == 15)
                  ws_done()
                  act(SG[i2].sl(), psr2(b), AF.Silu)
                  slu = ws_get("w_f1", 0, 16, FFN + jj * 128, 128)
                  b = banks(2)
                  for k in range(16):
                      w = slu.sl(k, wcol)
                      mm(psr(b), w, UT.sl(k, slice(128, 640)), k == 0, k == 15)
                      mm(psr(b + 1), w, UT.sl(k, slice(640, 1152)), k == 0, k == 15)
                  ws_done()
                  tt("dve", AB.sl(jl), psr2(b), SG[i2].sl(), ALU.mult)
              if gi < len(groups) - 1:
                  for nb in range(4):
                      slab = ws_get("w_f2", j0, G, nb * 512, 512)
                      for t in range(NT):
                          b = banks(1)
                          for k in range(G):
                              mm(psr(b), AB.sl(k, slice(t * 128, t * 128 + 128)), slab.sl(k), k == 0, k == G - 1)
                          hs = Hb.sl(t, slice(nb * 512, nb * 512 + 512))
                          tt("dve", hs, psr(b), hs, ALU.add)
                      ws_done()
              else:
                  fslabs = [ws_get("w_f2", j0, G, nb * 512, 512) for nb in range(4)]
                  for t in range(NT):
                      for nb in range(4):
                          b = banks(1)
                          for k in range(G):
                              mm(psr(b), AB.sl(k, slice(t * 128, t * 128 + 128)), fslabs[nb].sl(k), k == 0, k == G - 1)
                          hs = Hb.sl(t, slice(nb * 512, nb * 512 + 512))
                          tt("dve", hs, psr(b), hs, ALU.add)
                  ws_done(4)
              j0 += G

          chk("H", ps_)
          load_g(4)
          for t in range(NT):
              rsc = norm_rstd1(Hb.sl(t))
              stt("dve", Hb.sl(t), Hb.sl(t), rsc, GB.sl(), ALU.mult, ALU.mult)
              key = ("o", t)
              if key not in out_keys:
                  out_keys.append(key)
              r0 = ps_ * T + t * 128
              dma("sp", yout[r0:r0 + 128, :], Hb.full[:, t, :], reads=Hb.sl(t).toks, key=key)
    except _Stop:
        pass

    if seq_in is not None:
        P.emit(nc, final_wait_keys=out_keys)
    st.close()
    nc._memmap = memmap
    return nc, seq_rec


_CACHE = {}


def _get_nc():
    if "nc" not in _CACHE:
        _, seq = _build(None)
        nc, _ = _build(seq)
        _CACHE["nc"] = nc
    return _CACHE["nc"]


def _host_inputs(x, mem, g_mix, w_in, conv_w, attn_sinks, w_attn_proj, w_conv_proj, w_mix_out,
                 g_xattn, g_mem, w_xq, w_xkv, w_xo, g_ffn, w_ffn_in, w_ffn_out, g_final):
    f32 = np.float32
    x = np.asarray(x, f32)
    mem = np.asarray(mem, f32)
    perm = []
    for c in range(8):
        for r in range(2):
            hq = (2 * (c // 4) + r) * 4 + (c % 4)
            perm.extend(range(hq * 64, hq * 64 + 64))
    perm = np.asarray(perm)
    w_in0 = np.asarray(w_in, f32)[0]
    w_in_p = np.ascontiguousarray(np.concatenate([w_in0[:, perm], w_in0[:, 1024:]], axis=1))
    w_ap_p = np.ascontiguousarray(np.asarray(w_attn_proj, f32)[0][perm, :])
    gvec = np.ascontiguousarray(np.stack([np.asarray(g_mix, f32)[0], np.asarray(g_xattn, f32)[0],
                                          np.asarray(g_mem, f32)[0], np.asarray(g_ffn, f32)[0],
                                          np.asarray(g_final, f32)]))
    cw = np.asarray(conv_w, f32)[0]
    cwin = np.ascontiguousarray(cw.reshape(3, 8, 128).transpose(2, 1, 0).reshape(128, 24))
    sinkin = np.ascontiguousarray(np.asarray(attn_sinks, f32)[0].reshape(1, 16))
    ident = np.eye(128, dtype=f32).astype(ml_dtypes.bfloat16)
    jj = np.arange(128)[:, None]
    ii = np.arange(128)[None, :]
    gen = np.stack([(jj > ii), (jj <= ii)]).astype(f32)
    genb = ((gen - 1.0) * 30000.0).transpose(2, 0, 1)
    half = 32
    inv_freq = (f32(10000.0) ** (-np.arange(half, dtype=f32) / f32(half))).astype(f32)
    shared = {"gvec": gvec, "w_in": w_in_p, "w_ap": w_ap_p,
              "w_cp": np.ascontiguousarray(np.asarray(w_conv_proj, f32)[0]),
              "w_mix": np.ascontiguousarray(np.asarray(w_mix_out, f32)[0]),
              "w_xq": np.ascontiguousarray(np.asarray(w_xq, f32)[0]),
              "w_xkv": np.ascontiguousarray(np.asarray(w_xkv, f32)[0]),
              "w_xo": np.ascontiguousarray(np.asarray(w_xo, f32)[0]),
              "w_f1": np.ascontiguousarray(np.asarray(w_ffn_in, f32)[0]),
              "w_f2": np.ascontiguousarray(np.asarray(w_ffn_out, f32)[0]),
              "cwin": cwin, "sinkin": sinkin, "identin": ident}
    maps = []
    for c in range(NCORES):
        b = c // 2
        xin = np.zeros((NPASS, TH, D), f32)
        rope = np.zeros((NPASS, 128, 2, 9, 32), f32)
        masks = np.zeros((128, 3, 2, 128), f32)
        masks[:, 0] = genb
        for p in range(NPASS):
            start = (c % 2) * TOK_CORE + p * T
            if start > 0:
                xin[p, 0:128] = x[b, start - 128:start]
            xin[p, 128:] = x[b, start:start + T]
            pos = (start - 128 + np.arange(9 * 128)).astype(f32)
            ang = (pos[:, None] * inv_freq[None, :]).astype(f32)
            cs = np.cos(ang).astype(f32).reshape(9, 128, 32)
            sn = np.sin(ang).astype(f32).reshape(9, 128, 32)
            rope[p, :, 0] = cs.transpose(1, 0, 2)
            rope[p, :, 1] = sn.transpose(1, 0, 2)
            m = genb.copy()
            if start == 0:
                m[:, 0, :] = -30000.0
            masks[:, 1 + p] = m
        d = dict(shared)
        d["xin"] = xin
        d["memin"] = np.ascontiguousarray(mem[b])
        d["ropein"] = rope
        d["maskin"] = masks.astype(ml_dtypes.bfloat16)
        maps.append(d)
    return maps


def kernel(**inputs):
    maps = _host_inputs(**inputs)
    nc = _get_nc()
    res = run_bass_kernel_spmd(nc, maps, core_ids=list(range(NCORES)))
    out = np.empty((BATCH, SEQ, D), np.float32)
    for c in range(NCORES):
        b = c // 2
        s0 = (c % 2) * TOK_CORE
        out[b, s0:s0 + TOK_CORE] = res.results[c]["yout"]
    return out
```
